# Optimizing a Trainium2 kernel written in Bass

```python
import jax, jax.numpy as jnp
from jax import lax
import numpy as np

D_MODEL = 1024
BATCH = 4
SEQ = 8192
DEPTH = 1

D_FF = 2816
D_CONV = D_MODEL // 2
D_POOL = D_MODEL - D_CONV
CONV_WIDTH = 31
POOL_WINDOWS = (2, 4, 8, 16)
N_POOL_GROUPS = len(POOL_WINDOWS)
POOL_GROUP = D_POOL // N_POOL_GROUPS
D_IN = 2 * D_CONV + D_POOL
RMS_EPS = 1e-6
LN_EPS = 1e-5
FFN_RES_WEIGHT = 0.5

kernel_name = "macaron_conv_pool_hybrid_layer"


def _rmsnorm(x, g):
    xf = x.astype(jnp.float32)
    r = lax.rsqrt(jnp.mean(xf * xf, axis=-1, keepdims=True) + RMS_EPS)
    return (xf * r).astype(x.dtype) * g


def _layernorm(x, g, b):
    xf = x.astype(jnp.float32)
    mu = jnp.mean(xf, axis=-1, keepdims=True)
    var = jnp.mean(jnp.square(xf - mu), axis=-1, keepdims=True)
    return ((xf - mu) * lax.rsqrt(var + LN_EPS)).astype(x.dtype) * g + b


def _swiglu(h, w_gate, w_up, w_down):
    return (jax.nn.silu(h @ w_gate) * (h @ w_up)) @ w_down


def _causal_depthwise_conv(u, w, b):
    c = u.shape[-1]
    rhs = w.reshape(CONV_WIDTH, 1, c).astype(u.dtype)
    out = lax.conv_general_dilated(
        u, rhs, window_strides=(1,), padding=[(CONV_WIDTH - 1, 0)],
        dimension_numbers=("NWC", "WIO", "NWC"), feature_group_count=c)
    return out + b


def _multiscale_pool(p, pool_w, pool_scale):
    bsz, t, _ = p.shape
    pf = p.reshape(bsz, t, N_POOL_GROUPS, POOL_GROUP).astype(jnp.float32)
    cs = jnp.cumsum(pf, axis=1)
    pos = jnp.arange(t, dtype=jnp.float32)[None, :, None] + 1.0
    pooled = []
    for gi, w in enumerate(POOL_WINDOWS):
        csg = cs[:, :, gi]
        lag = jnp.pad(csg, ((0, 0), (w, 0), (0, 0)))[:, :t]
        cnt = jnp.minimum(pos, float(w))
        pooled.append((csg - lag) / cnt)
    mixed = (jnp.stack(pooled, axis=2) - pf).astype(p.dtype)
    out = jnp.einsum("btgc,gcd->btgd", mixed, pool_w)
    return out.reshape(bsz, t, D_POOL) * pool_scale


def setup_inputs(seed: int = 0) -> dict:
    key = jax.random.key(seed)
    ks = jax.random.split(key, 24)
    f32 = jnp.float32

    def nrm(k, shape, fan_in):
        return jax.random.normal(k, shape, f32) * (fan_in ** -0.5)

    def gain(k, n):
        return 1.0 + 0.02 * jax.random.normal(k, (n,), f32)

    def bias(k, n):
        return 0.02 * jax.random.normal(k, (n,), f32)

    return {
        "x": jax.random.normal(ks[0], (BATCH, SEQ, D_MODEL), f32),
        "ffn1_norm": gain(ks[1], D_MODEL),
        "ffn1_w_gate": nrm(ks[2], (D_MODEL, D_FF), D_MODEL),
        "ffn1_w_up": nrm(ks[3], (D_MODEL, D_FF), D_MODEL),
        "ffn1_w_down": nrm(ks[4], (D_FF, D_MODEL), D_FF),
        "mix_norm": gain(ks[5], D_MODEL),
        "w_in": nrm(ks[6], (D_MODEL, D_IN), D_MODEL),
        "conv_dw": nrm(ks[7], (CONV_WIDTH, D_CONV), CONV_WIDTH),
        "conv_dw_b": bias(ks[8], D_CONV),
        "conv_ln_g": gain(ks[9], D_CONV),
        "conv_ln_b": bias(ks[10], D_CONV),
        "conv_pw": nrm(ks[11], (D_CONV, D_CONV), D_CONV),
        "pool_w": nrm(ks[12], (N_POOL_GROUPS, POOL_GROUP, POOL_GROUP), POOL_GROUP),
        "pool_scale": gain(ks[13], D_POOL),
        "w_out": nrm(ks[14], (D_CONV + D_POOL, D_MODEL), D_CONV + D_POOL),
        "ffn2_norm": gain(ks[15], D_MODEL),
        "ffn2_w_gate": nrm(ks[16], (D_MODEL, D_FF), D_MODEL),
        "ffn2_w_up": nrm(ks[17], (D_MODEL, D_FF), D_MODEL),
        "ffn2_w_down": nrm(ks[18], (D_FF, D_MODEL), D_FF),
        "final_norm": gain(ks[19], D_MODEL),
    }


def reference(x, ffn1_norm, ffn1_w_gate, ffn1_w_up, ffn1_w_down, mix_norm, w_in,
              conv_dw, conv_dw_b, conv_ln_g, conv_ln_b, conv_pw, pool_w, pool_scale,
              w_out, ffn2_norm, ffn2_w_gate, ffn2_w_up, ffn2_w_down, final_norm):
    for _ in range(DEPTH):
        x = x + FFN_RES_WEIGHT * _swiglu(_rmsnorm(x, ffn1_norm), ffn1_w_gate, ffn1_w_up, ffn1_w_down)

        h = _rmsnorm(x, mix_norm)
        proj = h @ w_in
        a = proj[..., :D_CONV]
        g = proj[..., D_CONV:2 * D_CONV]
        p = proj[..., 2 * D_CONV:]

        u = a * jax.nn.sigmoid(g)
        u = _causal_depthwise_conv(u, conv_dw, conv_dw_b)
        u = jax.nn.silu(_layernorm(u, conv_ln_g, conv_ln_b))
        conv_out = u @ conv_pw

        pool_out = _multiscale_pool(p, pool_w, pool_scale)

        x = x + jnp.concatenate([conv_out, pool_out], axis=-1) @ w_out

        x = x + FFN_RES_WEIGHT * _swiglu(_rmsnorm(x, ffn2_norm), ffn2_w_gate, ffn2_w_up, ffn2_w_down)
    return _rmsnorm(x, final_norm)
```

```python
import numpy as np
import concourse.bass as bass
import concourse.mybir as mybir
from concourse.bass_utils import run_bass_kernel_spmd

F32 = mybir.dt.float32
BF16 = mybir.dt.bfloat16
AF = mybir.ActivationFunctionType
ALU = mybir.AluOpType

D_MODEL = 1024
D_FF = 2816
D_CONV = 512
CONV_W = 31
KC = 8
FCH = 11
CC = 4
HALO = 32
SUBW = 512
TT = 1024
NSLOT = 6
RSQ = "lnexp"
RMS_EPS = 1e-6
LN_EPS = 1e-5

V_FFN1, V_MIX, V_FFN2, V_FIN = 0, 8, 16, 24
V_CB, V_LNG, V_LNB, V_PSC = 32, 36, 40, 44
V_DW = 48
NPAIR = 16
NV = V_DW + CC * 2 * NPAIR

COMPUTE = ("pe", "act", "dve", "pool")


class Buf:
    __slots__ = ("name", "lastw", "readers", "psum")

    def __init__(self, name, psum=False):
        self.name = name
        self.lastw = None
        self.readers = []
        self.psum = psum


class Sched:
    def __init__(self, nc, n_dma_sems=8):
        self.nc = nc
        self.prog = {e: [] for e in COMPUTE + ("sp",)}
        self.sem = {e: nc.alloc_semaphore(f"s_{e}") for e in COMPUTE}
        self.cnt = {e: 0 for e in COMPUTE}
        self.waited = {e: {} for e in COMPUTE + ("sp",)}
        self.dma_sems = {}
        self.n_dma_sems = n_dma_sems
        self.dma_rr = {}

    def _need(self, eng, tok, waits):
        if tok is None:
            return
        key, sem, val = tok
        if self.waited[eng].get(key, 0) >= val:
            return
        self.waited[eng][key] = val
        waits[key] = (sem, val)

    def _deps(self, eng, reads, writes):
        waits = {}
        for b in reads:
            self._need(eng, b.lastw, waits)
            if b.psum:
                for t in b.readers:
                    if t[0] != eng:
                        self._need(eng, t, waits)
        for b in writes:
            self._need(eng, b.lastw, waits)
            for t in b.readers:
                self._need(eng, t, waits)
        return list(waits.values())

    def _commit(self, tok, reads, writes):
        for b in reads:
            b.readers.append(tok)
            if len(b.readers) > 12:
                best = {}
                for t in b.readers:
                    if t[0] not in best or best[t[0]][2] < t[2]:
                        best[t[0]] = t
                b.readers = list(best.values())
        for b in writes:
            b.lastw = tok
            b.readers = []

    def op(self, eng, fn, reads=(), writes=()):
        waits = self._deps(eng, reads, writes)
        self.cnt[eng] += 1
        tok = (eng, self.sem[eng], self.cnt[eng])
        self.prog[eng].append((waits, fn, self.sem[eng]))
        self._commit(tok, reads, writes)
        return tok

    def dma(self, eng, fn, reads=(), writes=(), group="g"):
        k = (eng, group)
        if k not in self.dma_sems:
            self.dma_sems[k] = [[self.nc.alloc_semaphore(f"d_{eng}_{group}_{i}"), 0]
                                for i in range(self.n_dma_sems)]
            self.dma_rr[k] = 0
        i = self.dma_rr[k]
        self.dma_rr[k] = (i + 1) % len(self.dma_sems[k])
        ent = self.dma_sems[k][i]
        sem, prev = ent
        waits = self._deps(eng, reads, writes)
        if prev > 0:
            w2 = {}
            self._need(eng, ((eng, group, i), sem, prev), w2)
            waits += list(w2.values())
        ent[1] = prev + 16
        tok = ((eng, group, i), sem, prev + 16)
        self.prog[eng].append((waits, fn, ("dma", sem)))
        self._commit(tok, reads, writes)
        return tok

    def emit(self, final_group=None):
        names = {"pe": "tensor", "act": "scalar", "dve": "vector", "pool": "gpsimd", "sp": "sync"}
        final_waits = []
        if final_group is not None:
            final_waits = [(None, sem, val) for (sem, val) in self.dma_sems[final_group] if val > 0]
        with self.nc.Block() as block:
            for e, attr in names.items():
                prog = self.prog[e]
                fw = final_waits if e == "sp" else ()

                def section(engine, prog=prog, fw=fw):
                    for waits, fn, sem in prog:
                        for (s, v) in waits:
                            engine.wait_ge(s, v)
                        ins = fn(engine)
                        if isinstance(sem, tuple):
                            ins.then_inc(sem[1], 16)
                        else:
                            ins.then_inc(sem, 1)
                    for (_k, s, v) in fw:
                        engine.wait_ge(s, v)

                getattr(block, attr)(section)


class Sub:
    def __init__(self, name, c0, n, tok0, main):
        self.name, self.c0, self.n, self.tok0, self.main = name, c0, n, tok0, main
        self.sl = slice(c0, c0 + n)


def build_program(ntiles=4, bcast_diag=True, debug_stop=None):
    TOK = ntiles * TT
    nc = bass.Bass("TRN2", target_bir_lowering=False)
    S = Sched(nc)

    def din(name, shape):
        return nc.dram_tensor(name, list(shape), F32, kind="ExternalInput").ap()

    xT_d = din("xT", [D_MODEL, HALO + TOK])
    pos_d = din("pos", [128, 16])
    vecs_d = din("vecs", [128, NV])
    ident_d = din("ident", [128, 64])
    W = {}
    for f in ("ffn1", "ffn2"):
        W[f + "_g"] = din(f + "_w_gate", [D_MODEL, D_FF])
        W[f + "_u"] = din(f + "_w_up", [D_MODEL, D_FF])
        W[f + "_d"] = din(f + "_w_down", [D_FF, D_MODEL])
    W["w_in"] = din("w_in", [D_MODEL, 3 * D_CONV])
    W["conv_pw"] = din("conv_pw", [D_CONV, D_CONV])
    W["pool_w"] = din("pool_w", [4 * 128, 128])
    W["w_out"] = din("w_out", [D_MODEL, D_MODEL])
    outT_d = nc.dram_tensor("outT", [D_MODEL, TOK], F32, kind="ExternalOutput").ap()

    WCOLS = HALO + TT
    sb = nc.alloc_sbuf_tensor
    xT = sb("xT_sb", [128, KC, WCOLS], F32)
    hT = sb("hT_sb", [128, KC, WCOLS], BF16)
    actT = sb("actT_sb", [128, FCH, WCOLS], BF16)
    wring = [sb(f"wr{i}", [128, 8, 512], BF16) for i in range(NSLOT)]
    wring_b = [Buf(f"wr{i}") for i in range(NSLOT)]
    NBF = 4
    bfr = [sb(f"bfr{i}", [128, SUBW], BF16) for i in range(NBF)]
    bfr_b = [Buf(f"bfr{i}") for i in range(NBF)]
    NTMP = 3
    tmpf = [sb(f"tmpf{i}", [128, SUBW], F32) for i in range(NTMP)]
    tmpf_b = [Buf(f"tmpf{i}") for i in range(NTMP)]
    NST = 2
    stt = [sb(f"stt{i}", [128, SUBW], F32) for i in range(NST)]
    stt_b = [Buf(f"stt{i}") for i in range(NST)]
    lnM = sb("lnM", [128, SUBW], F32)
    lnV = sb("lnV", [128, SUBW], F32)
    lnM_b, lnV_b = Buf("lnM"), Buf("lnV")
    ubuf = [sb(f"ubuf{i}", [128, CC, 2, HALO + SUBW], BF16) for i in range(2)]
    ubuf_b = [[Buf(f"u{i}_{c}") for c in range(CC)] for i in range(2)]
    urep_b = [[Buf(f"ur{i}_{c}") for c in range(CC)] for i in range(2)]
    uhist_b = [Buf(f"uh{i}") for i in range(2)]
    pbuf = [sb(f"pbuf{i}", [128, CC, 16 + SUBW], F32) for i in range(2)]
    pbuf_b = [[Buf(f"p{i}_{c}") for c in range(CC)] for i in range(2)]
    phist_b = [Buf(f"ph{i}") for i in range(2)]
    psc = [sb(f"psc{i}", [128, 16 + SUBW], F32) for i in range(2)]
    psc_b = [Buf(f"psc{i}") for i in range(2)]
    convo = sb("convo", [128, CC, SUBW], F32)
    convo_b = [Buf(f"convo{c}") for c in range(CC)]
    yb = sb("yb", [128, CC, SUBW], BF16)
    yb_b = [Buf(f"yb{c}") for c in range(CC)]
    NMX = 4
    mixed = [sb(f"mixed{i}", [128, SUBW], BF16) for i in range(NMX)]
    mixed_b = [Buf(f"mixed{i}") for i in range(NMX)]
    catT = sb("catT", [128, 2 * CC, SUBW], BF16)
    catT_b = [Buf(f"cat{c}") for c in range(2 * CC)]
    NDG = 3
    NDGD, NDGP = 2, 4
    dgr = [sb(f"dgr{i}", [128, 8, 64], BF16) for i in range(NDGD)]
    dgr_b = [Buf(f"dgr{i}") for i in range(NDGD)]
    dgp = [sb(f"dgp{i}", [128, 8, 64], BF16) for i in range(NDGP)]
    dgp_b = [Buf(f"dgp{i}") for i in range(NDGP)]
    vecs = sb("vecs_sb", [128, NV], F32)
    vecs_b = Buf("vecs")
    wh = sb("wh_sb", [128, CC * 2 * NPAIR], F32)
    wh_b = Buf("wh")
    ones = sb("ones_sb", [128, 128], BF16)
    ones_b = Buf("ones")
    identf = sb("identf_sb", [128, 64], F32)
    ident_b = Buf("ident")
    epsc = sb("epsc_sb", [128, 2], F32)
    negh_b = Buf("epsc")
    rcnt = sb("rcnt_sb", [128, CC, 16], F32)
    rcnt_b = Buf("rcnt")

    NBANK = 8
    banks = [nc.alloc_psum_tensor(f"bank{i}", [128, SUBW], F32) for i in range(NBANK)]
    banks_b = [Buf(f"bank{i}", psum=True) for i in range(NBANK)]
    rr = {"bank": 0, "bfr": 0, "tmp": 0, "stt": 0, "mixed": 0, "dgr": 0, "dgp": 0, "psc": 0}

    def nxt(kind, n):
        i = rr[kind]
        rr[kind] = (i + 1) % n
        return i

    held = set()

    def bank(hold=False):
        for _ in range(NBANK):
            i = nxt("bank", NBANK)
            if i not in held:
                break
        else:
            raise RuntimeError("all PSUM banks held")
        if hold:
            held.add(i)
        return banks[i], banks_b[i]

    def release(bk_b):
        held.discard(banks_b.index(bk_b))

    S.dma("sp", lambda e: e.dma_start(out=vecs[:], in_=vecs_d), writes=[vecs_b], group="c")
    posb, pos_b = lnV[:, 0:16], lnV_b
    fix, fix_b = lnM[:, 0:16], lnM_b
    S.dma("sp", lambda e: e.dma_start(out=posb, in_=pos_d), writes=[pos_b], group="c")
    S.op("dve", lambda v: v.memset(ones[:], 1.0), writes=[ones_b])
    S.op("dve", lambda v: v.memset(epsc[:, 0:1], RMS_EPS), writes=[negh_b])
    S.op("dve", lambda v: v.memset(epsc[:, 1:2], LN_EPS), reads=[negh_b], writes=[negh_b])
    S.dma("sp", lambda e: e.dma_start(out=identf[:], in_=ident_d), writes=[ident_b], group="c")
    for i in range(2):
        S.op("dve", lambda v, i=i: v.memset(psc[i][:], 0.0), writes=[psc_b[i]])
        S.op("pool", lambda g, i=i: g.memset(ubuf[i][:], 0.0),
             writes=ubuf_b[i] + urep_b[i] + [uhist_b[i]])
    S.op("dve", lambda v: v.tensor_scalar(out=wh[:], in0=vecs[:, V_DW:V_DW + CC * 2 * NPAIR], scalar1=0.5,
                                          scalar2=None, op0=ALU.mult), reads=[vecs_b], writes=[wh_b])
    for g in range(CC):
        S.op("dve", lambda v, g=g: v.tensor_scalar(out=rcnt[:, g, :], in0=posb, scalar1=1.0,
                                                   scalar2=float(2 ** (g + 1)), op0=ALU.add, op1=ALU.min),
             reads=[pos_b], writes=[rcnt_b])
    S.op("dve", lambda v: v.reciprocal(out=rcnt[:], in_=rcnt[:]), reads=[rcnt_b], writes=[rcnt_b])

    def wspecs_ffn(f):
        specs = []
        for half in range(2):
            f0 = half * FCH
            groups = [(g0, min(4, FCH - g0)) for g0 in range(0, FCH, 4)]
            for q, (g0, nf) in enumerate(groups):
                if half == 0 and q < 2:
                    rels = (4 - 2 * q, 3 - 2 * q)
                else:
                    rels = (2, 1)
                for nm, rel in zip(("_g", "_u"), rels):
                    specs.append(((f + nm, 0, 8, (f0 + g0) * 128, nf * 128), rel))
            for dg in range(2):
                for k0, r in ((0, 2), (8, 1)):
                    nk = min(8, FCH - k0)
                    rel = r if half == 0 else (4 - 2 * dg - (0 if k0 == 0 else 1))
                    specs.append(((f + "_d", (f0 + k0) * 128, nk, dg * 512, 512), rel))
        return specs

    def wspecs_mixer():
        return [(("w_in", 0, 8, 0, 512), 3), (("w_in", 0, 8, 512, 512), 2), (("w_in", 0, 8, 1024, 512), 1),
                (("conv_pw", 0, 4, 0, 512), 4), (("pool_w", 0, 4, 0, 128), 3),
                (("w_out", 0, 8, 0, 512), 2), (("w_out", 0, 8, 512, 512), 1)]

    wlist = []
    for _j in range(ntiles):
        wlist += wspecs_ffn("ffn1") + wspecs_mixer() + wspecs_ffn("ffn2")
    wstate = {"loaded": 0, "cons": 0}
    startup_gate = []

    def w_issue(i):
        name, r0, nk, c0, ncols = wlist[i][0]
        slot = i % NSLOT
        src = W[name][r0:r0 + nk * 128, c0:c0 + ncols].rearrange("(k p) c -> p k c", p=128)
        dst = wring[slot][:, 0:nk, 0:ncols]
        gate = list(startup_gate) if 4 <= i < NSLOT else []
        S.dma("pool", lambda e, dst=dst, src=src: e.dma_start(out=dst, in_=src),
              reads=gate, writes=[wring_b[slot]], group="w")

    def wtile(spec):
        i = wstate["cons"]
        assert wlist[i][0] == spec, (i, wlist[i], spec)
        while wstate["loaded"] < min(len(wlist), i + NSLOT):
            m = wstate["loaded"]
            if m >= NSLOT:
                k = m - NSLOT
                if k + wlist[k][1] > i:
                    break
            w_issue(m)
            wstate["loaded"] += 1
        assert wstate["loaded"] > i
        wstate["cons"] += 1
        return wring[i % NSLOT], wring_b[i % NSLOT]

    class SB:
        pass

    def mk_sub_bufs(sub):
        o = SB()
        o.x = [Buf(f"x{c}{sub.name}") for c in range(KC)]
        o.h = [Buf(f"h{c}{sub.name}") for c in range(KC)]
        o.a = [Buf(f"a{c}{sub.name}") for c in range(FCH)]
        return o

    def mm_group(ps_ap, pairs, reads, bank_b):
        def fn(t, pairs=pairs, ps_ap=ps_ap):
            ins = None
            n = len(pairs)
            for i, (l, r) in enumerate(pairs):
                ins = t.matmul(ps_ap, lhsT=l, rhs=r, start=(i == 0), stop=(i == n - 1))
            return ins
        S.op("pe", fn, reads=reads, writes=[bank_b])

    def norm_sums(sub, sbufs):
        n = sub.n
        bk, bk_b = bank(hold=True)
        for c in range(KC):
            i = nxt("bfr", NBF)
            S.op("act", lambda a, i=i, c=c: a.activation(out=bfr[i][:, 0:n], in_=xT[:, c, sub.sl], func=AF.Square),
                 reads=[sbufs.x[c]], writes=[bfr_b[i]])
            S.op("pe", lambda t, i=i, c=c: t.matmul(bk[:, 0:n], lhsT=ones[:], rhs=bfr[i][:, 0:n],
                                                    start=(c == 0), stop=(c == KC - 1)),
                 reads=[bfr_b[i], ones_b], writes=[bk_b])
        return bk, bk_b

    def norm_rstd(sub, bk, bk_b):
        n = sub.n
        si = nxt("stt", NST)
        S.op("act", lambda a: a.activation(out=stt[si][:, 0:n], in_=bk[:, 0:n], func=AF.Ln,
                                           scale=1.0 / D_MODEL, bias=epsc[:, 0:1]),
             reads=[bk_b, negh_b], writes=[stt_b[si]])
        release(bk_b)
        S.op("act", lambda a: a.activation(out=stt[si][:, 0:n], in_=stt[si][:, 0:n], func=AF.Exp, scale=-0.5),
             reads=[stt_b[si]], writes=[stt_b[si]])
        return stt[si], stt_b[si]

    def norm_stats(sub, sbufs):
        bk, bk_b = norm_sums(sub, sbufs)
        return norm_rstd(sub, bk, bk_b)

    def h_ops(sub, sbufs, vcol, r, r_b):
        n = sub.n
        for c in range(KC):
            S.op("dve", lambda v, c=c: v.scalar_tensor_tensor(out=hT[:, c, sub.sl], in0=xT[:, c, sub.sl],
                                                              scalar=vecs[:, vcol + c:vcol + c + 1], in1=r[:, 0:n],
                                                              op0=ALU.mult, op1=ALU.mult),
                 reads=[sbufs.x[c], r_b, vecs_b], writes=[sbufs.h[c]])

    def norm_to_h_2(sub, sbufs, vcol):
        bk, bk_b = norm_sums(sub, sbufs)

        def part2():
            r, r_b = norm_rstd(sub, bk, bk_b)
            h_ops(sub, sbufs, vcol, r, r_b)
        return part2

    def norm_to_h(sub, sbufs, vcol):
        r, r_b = norm_stats(sub, sbufs)
        h_ops(sub, sbufs, vcol, r, r_b)

    pending = []

    def flush():
        while pending:
            r_ = pending.pop(0)()
            if callable(r_):
                r_()

    def ffn(f, subs, hook=None):
        def up_chunk(sub, sbufs, wg, wg_b, wu, wu_b, fi, fl):
            n = sub.n
            bg, bg_b = bank()
            mm_group(bg[:, 0:n], [(wg[:, k, fi * 128:(fi + 1) * 128], hT[:, k, sub.sl]) for k in range(KC)],
                     [wg_b] + sbufs.h, bg_b)
            bu, bu_b = bank()
            mm_group(bu[:, 0:n], [(wu[:, k, fi * 128:(fi + 1) * 128], hT[:, k, sub.sl]) for k in range(KC)],
                     [wu_b] + sbufs.h, bu_b)
            ti = nxt("tmp", NTMP)
            S.op("act", lambda a: a.activation(out=tmpf[ti][:, 0:n], in_=bg[:, 0:n], func=AF.Silu),
                 reads=[bg_b], writes=[tmpf_b[ti]])
            S.op("dve", lambda v: v.tensor_tensor(out=actT[:, fl, sub.sl], in0=bu[:, 0:n], in1=tmpf[ti][:, 0:n],
                                                  op=ALU.mult),
                 reads=[bu_b, tmpf_b[ti]], writes=[sbufs.a[fl]])

        def down_chunk(sub, sbufs, wd0, wd0_b, wd1, wd1_b, dg, dl):
            n = sub.n
            dc = dg * 4 + dl
            bk, bk_b = bank()
            pairs = []
            for fl in range(FCH):
                wt = wd0 if fl < 8 else wd1
                pairs.append((wt[:, fl % 8, dl * 128:(dl + 1) * 128], actT[:, fl, sub.sl]))
            mm_group(bk[:, 0:n], pairs, [wd0_b, wd1_b] + sbufs.a, bk_b)
            S.op("dve", lambda v: v.scalar_tensor_tensor(out=xT[:, dc, sub.sl], in0=bk[:, 0:n], scalar=0.5,
                                                         in1=xT[:, dc, sub.sl], op0=ALU.mult, op1=ALU.add),
                 reads=[bk_b, sbufs.x[dc]], writes=[sbufs.x[dc]])

        for half in range(2):
            f0 = half * FCH
            groups = []
            for g0 in range(0, FCH, 4):
                nf = min(4, FCH - g0)
                groups.append((g0, nf))
            gi = 0
            if half == 0:
                tl = []
                for (g0, nf) in groups[:2]:
                    tl.append(wtile((f + "_g", 0, 8, (f0 + g0) * 128, nf * 128)))
                    tl.append(wtile((f + "_u", 0, 8, (f0 + g0) * 128, nf * 128)))
                for (sub, sbufs) in subs:
                    for q, (g0, nf) in enumerate(groups[:2]):
                        (wg, wg_b), (wu, wu_b) = tl[2 * q], tl[2 * q + 1]
                        for fi in range(nf):
                            up_chunk(sub, sbufs, wg, wg_b, wu, wu_b, fi, g0 + fi)
                            if sub.main and q == 1 and fi == 1:
                                flush()
                gi = 2
            for (g0, nf) in groups[gi:]:
                wg, wg_b = wtile((f + "_g", 0, 8, (f0 + g0) * 128, nf * 128))
                wu, wu_b = wtile((f + "_u", 0, 8, (f0 + g0) * 128, nf * 128))
                for (sub, sbufs) in subs:
                    for fi in range(nf):
                        up_chunk(sub, sbufs, wg, wg_b, wu, wu_b, fi, g0 + fi)
            if half == 0:
                for dg in range(2):
                    wd0, wd0_b = wtile((f + "_d", (f0 + 0) * 128, 8, dg * 512, 512))
                    wd1, wd1_b = wtile((f + "_d", (f0 + 8) * 128, FCH - 8, dg * 512, 512))
                    for (sub, sbufs) in subs:
                        for dl in range(4):
                            down_chunk(sub, sbufs, wd0, wd0_b, wd1, wd1_b, dg, dl)
            else:
                tl = []
                for dg in range(2):
                    tl.append(wtile((f + "_d", (f0 + 0) * 128, 8, dg * 512, 512)))
                    tl.append(wtile((f + "_d", (f0 + 8) * 128, FCH - 8, dg * 512, 512)))
                for (sub, sbufs) in subs:
                    for dg in range(2):
                        (wd0, wd0_b), (wd1, wd1_b) = tl[2 * dg], tl[2 * dg + 1]
                        for dl in range(4):
                            down_chunk(sub, sbufs, wd0, wd0_b, wd1, wd1_b, dg, dl)
                            if dg == 1 and dl == 1 and sub.main:
                                flush()
                    if hook is not None:
                        p2 = hook(sub, sbufs)
                        if p2 is not None:
                            pending.append(p2)

    def mixer(subs, first_of_core, hook):
        wa, wa_b = wtile(("w_in", 0, 8, 0, 512))
        wg, wg_b = wtile(("w_in", 0, 8, 512, 512))
        wp, wp_b = wtile(("w_in", 0, 8, 1024, 512))
        first_main = [True]
        deferred = []

        def hist_copy(ui):
            nu = 1 - ui
            S.op("pool", lambda g: g.tensor_copy(out=ubuf[nu][0:64, :, 0, 0:HALO],
                                                 in_=ubuf[ui][0:64, :, 0, SUBW:SUBW + HALO]),
                 reads=ubuf_b[ui], writes=[uhist_b[nu]])
            S.op("pool", lambda g: g.tensor_copy(out=ubuf[nu][64:128, :, 1, 0:HALO],
                                                 in_=ubuf[ui][64:128, :, 1, SUBW:SUBW + HALO]),
                 reads=ubuf_b[ui], writes=[uhist_b[nu]])
            S.op("pool", lambda g: g.tensor_copy(out=pbuf[nu][:, :, 0:16], in_=pbuf[ui][:, :, SUBW:SUBW + 16]),
                 reads=pbuf_b[ui], writes=[phist_b[nu]])

        msubs = [s_ for s_ in subs if s_[0].main]
        sts = [MixState(sub, sbufs, ui) for (sub, sbufs, ui) in msubs]
        A_, B_ = sts
        for (sub, sbufs, ui) in subs:
            n = sub.n
            ucol = HALO if sub.main else 0
            for c in range(CC):
                ba, ba_b = bank()
                mm_group(ba[:, 0:n], [(wa[:, k, c * 128:(c + 1) * 128], hT[:, k, sub.sl]) for k in range(KC)],
                         [wa_b] + sbufs.h, ba_b)
                bg, bg_b = bank()
                mm_group(bg[:, 0:n], [(wg[:, k, c * 128:(c + 1) * 128], hT[:, k, sub.sl]) for k in range(KC)],
                         [wg_b] + sbufs.h, bg_b)
                ti = nxt("tmp", NTMP)
                S.op("act", lambda a, ti=ti, bg=bg, n=n: a.activation(out=tmpf[ti][:, 0:n], in_=bg[:, 0:n],
                                                                      func=AF.Tanh, scale=0.5),
                     reads=[bg_b], writes=[tmpf_b[ti]])
                uw = [ubuf_b[ui][c]] if sub.main else [uhist_b[ui]]
                for hh in range(2):
                    ps_ = slice(64 * hh, 64 * hh + 64)
                    S.op("dve", lambda v, ti=ti, ba=ba, n=n, c=c, ui=ui, ucol=ucol, hh=hh, ps_=ps_: v.scalar_tensor_tensor(
                        out=ubuf[ui][ps_, c, hh, ucol:ucol + n], in0=tmpf[ti][ps_, 0:n], scalar=1.0, in1=ba[ps_, 0:n],
                        op0=ALU.add, op1=ALU.mult),
                         reads=[ba_b, tmpf_b[ti]], writes=uw)
                if sub.main:
                    WU = HALO + SUBW
                    S.dma("sp", lambda e, c=c, ui=ui: e.dma_start(out=ubuf[ui][64:128, c, 0, 0:WU - 1],
                                                                  in_=ubuf[ui][0:64, c, 0, 1:WU]),
                          reads=[ubuf_b[ui][c], uhist_b[ui]], writes=[urep_b[ui][c]], group="r")
                    S.dma("sp", lambda e, c=c, ui=ui: e.dma_start(out=ubuf[ui][0:64, c, 1, 0:WU - 1],
                                                                  in_=ubuf[ui][64:128, c, 1, 1:WU]),
                          reads=[ubuf_b[ui][c], uhist_b[ui], urep_b[ui][c]], writes=[urep_b[ui][c]], group="r")
                if sub.main and first_main[0] and c == 1:
                    flush()
            for c in range(CC):
                bp, bp_b = bank()
                mm_group(bp[:, 0:n], [(wp[:, k, c * 128:(c + 1) * 128], hT[:, k, sub.sl]) for k in range(KC)],
                         [wp_b] + sbufs.h, bp_b)
                if sub.main:
                    S.op("act", lambda a, bp=bp, c=c, ui=ui, n=n: a.activation(
                        out=pbuf[ui][:, c, 16:16 + n], in_=bp[:, 0:n], func=AF.Identity),
                         reads=[bp_b], writes=[pbuf_b[ui][c]])
                else:
                    S.op("act", lambda a, bp=bp, c=c, ui=ui: a.activation(
                        out=pbuf[ui][:, c, 0:16], in_=bp[:, 16:32], func=AF.Identity),
                         reads=[bp_b], writes=[phist_b[ui]])
            if sub.main:
                if first_main[0]:
                    hist_copy(ui)
                    first_main[0] = False
                    pool_dve(A_, first_of_core, eng="pool")
                else:
                    deferred.append(ui)
        wpw, wpw_b = wtile(("conv_pw", 0, 4, 0, 512))
        wpl, wpl_b = wtile(("pool_w", 0, 4, 0, 128))
        wos = [wtile(("w_out", 0, 8, 0, 512)), wtile(("w_out", 0, 8, 512, 512))]
        for c in range(CC):
            conv_chunk(A_, c)
        B_.all_pool = True
        conv_chunk(B_, 0)
        evac_part(A_)
        ln_apply(A_)
        conv_chunk(B_, 1)
        conv_chunk(B_, 2)
        conv_chunk(B_, 3)
        pw_stage(A_, wpw, wpw_b, evac=False)
        evac_part(B_)
        pw_evac(A_)
        pool_pe(A_, wpl, wpl_b)
        ln_apply(B_)
        pool_dve(B_, False, groups=(3,))
        wout_stage(A_, wos)
        pool_dve(B_, False, groups=(2, 1, 0))
        p2 = hook(A_.sub, A_.sbufs)
        while deferred:
            hist_copy(deferred.pop(0))
        pw_stage(B_, wpw, wpw_b)
        p2b = p2() if p2 is not None else None
        pool_pe(B_, wpl, wpl_b)
        if callable(p2b):
            p2b()
        wout_stage(B_, wos)
        p2 = hook(B_.sub, B_.sbufs)
        if p2 is not None:
            pending.append(p2)

    class MixState:
        def __init__(self, sub, sbufs, ui):
            self.sub, self.sbufs, self.ui, self.n = sub, sbufs, ui, sub.n
            self.cbanks = [None] * CC
            self.done_c = 0
            self.all_pool = False

    def conv_chunk(st, c):
        n, ui = st.n, st.ui
        bk, bk_b = bank(hold=True)
        st.cbanks[c] = (bk, bk_b)
        for r in range(2):
            tl = []
            for hh in range(2):
                on_pool = (st.all_pool and c >= 1) or (hh == 1)
                if on_pool:
                    di = nxt("dgp", NDGP)
                    dt_, dt_b = dgp[di], dgp_b[di]
                else:
                    di = nxt("dgr", NDGD)
                    dt_, dt_b = dgr[di], dgr_b[di]
                base = (c * 2 + hh) * NPAIR + r * 8
                wsl = wh[:, base:base + 8]
                S.op("pool" if on_pool else "dve", lambda v, dt_=dt_, wsl=wsl: v.tensor_tensor(
                    out=dt_[:], in0=identf[:].unsqueeze(1).to_broadcast([128, 8, 64]),
                    in1=wsl.unsqueeze(2).to_broadcast([128, 8, 64]), op=ALU.mult),
                     reads=[ident_b, wh_b], writes=[dt_b])
                tl.append((dt_, dt_b))

            def fn(t, tl=tl, r=r):
                ins = None
                for jl in range(8):
                    jj = r * 8 + jl
                    for hh in range(2):
                        ins = t.matmul(bk[64 * hh:64 * hh + 64, 0:n], lhsT=tl[hh][0][:, jl, :],
                                       rhs=ubuf[ui][:, c, hh, 2 + 2 * jj:2 + 2 * jj + n],
                                       start=(jj == 0), stop=(jj == NPAIR - 1), tile_position=(0, 64 * hh))
                return ins
            S.op("pe", fn, reads=[tl[0][1], tl[1][1], ubuf_b[ui][c], urep_b[ui][c], uhist_b[ui]], writes=[bk_b])

    def evac_part(st):
        n = st.n
        st.s1 = None
        st.bfi = []
        for c in range(CC):
            bk, bk_b = st.cbanks[c]
            bcol = vecs[:, V_CB + c:V_CB + c + 1]
            i1 = nxt("bfr", NBF)
            S.op("act", lambda a, bk=bk, i1=i1, bcol=bcol: a.activation(out=bfr[i1][:, 0:n], in_=bk[:, 0:n],
                                                                        func=AF.Identity, bias=bcol),
                 reads=[bk_b, vecs_b], writes=[bfr_b[i1]])
            i2 = nxt("bfr", NBF)
            S.op("act", lambda a, bk=bk, i2=i2, bcol=bcol: a.activation(out=bfr[i2][:, 0:n], in_=bk[:, 0:n],
                                                                        func=AF.Square, bias=bcol),
                 reads=[bk_b, vecs_b], writes=[bfr_b[i2]])
            S.op("dve", lambda v, bk=bk, c=c, bcol=bcol: v.tensor_scalar(out=convo[:, c, 0:n], in0=bk[:, 0:n],
                                                                         scalar1=bcol, scalar2=None, op0=ALU.add),
                 reads=[bk_b, vecs_b, bfr_b[i1], bfr_b[i2]], writes=[convo_b[c]])
            release(bk_b)
            st.bfi.append((i1, i2))
            if len(st.bfi) == 2 or c == CC - 1:
                if st.s1 is None:
                    st.s1, st.s1_b = bank(hold=True)
                    st.s2, st.s2_b = bank(hold=True)
                stats_pe(st)

    def stats_pe(st):
        n = st.n
        s1, s1_b, s2, s2_b = st.s1, st.s1_b, st.s2, st.s2_b
        while st.bfi:
            c = st.done_c
            i1, i2 = st.bfi.pop(0)
            S.op("pe", lambda t, i1=i1, c=c: t.matmul(s1[:, 0:n], lhsT=ones[:], rhs=bfr[i1][:, 0:n],
                                                      start=(c == 0), stop=(c == CC - 1)),
                 reads=[bfr_b[i1], ones_b], writes=[s1_b])
            S.op("pe", lambda t, i2=i2, c=c: t.matmul(s2[:, 0:n], lhsT=ones[:], rhs=bfr[i2][:, 0:n],
                                                      start=(c == 0), stop=(c == CC - 1)),
                 reads=[bfr_b[i2], ones_b], writes=[s2_b])
            st.done_c += 1

    def ln_apply(st):
        n = st.n
        s1, s1_b, s2, s2_b = st.s1, st.s1_b, st.s2, st.s2_b
        S.op("dve", lambda v: v.tensor_scalar(out=lnM[:, 0:n], in0=s1[:, 0:n], scalar1=1.0 / D_CONV, scalar2=None,
                                              op0=ALU.mult), reads=[s1_b], writes=[lnM_b])
        S.op("dve", lambda v: v.tensor_tensor(out=lnV[:, 0:n], in0=lnM[:, 0:n], in1=lnM[:, 0:n], op=ALU.mult),
             reads=[lnM_b], writes=[lnV_b])
        S.op("dve", lambda v: v.scalar_tensor_tensor(out=lnV[:, 0:n], in0=s2[:, 0:n], scalar=1.0 / D_CONV,
                                                     in1=lnV[:, 0:n], op0=ALU.mult, op1=ALU.subtract),
             reads=[s2_b, lnV_b], writes=[lnV_b])
        release(s1_b)
        release(s2_b)
        if RSQ == "lnexp":
            S.op("act", lambda a: a.activation(out=lnV[:, 0:n], in_=lnV[:, 0:n], func=AF.Ln, bias=epsc[:, 1:2]),
                 reads=[lnV_b, negh_b], writes=[lnV_b])
            S.op("act", lambda a: a.activation(out=lnV[:, 0:n], in_=lnV[:, 0:n], func=AF.Exp, scale=-0.5),
                 reads=[lnV_b], writes=[lnV_b])
        else:
            S.op("act", lambda a: a.activation(out=lnV[:, 0:n], in_=lnV[:, 0:n], func=AF.Sqrt, bias=epsc[:, 1:2]),
                 reads=[lnV_b, negh_b], writes=[lnV_b])
            S.op("dve", lambda v: v.reciprocal(out=lnV[:, 0:n], in_=lnV[:, 0:n]), reads=[lnV_b], writes=[lnV_b])
        S.op("dve", lambda v: v.scalar_tensor_tensor(out=lnM[:, 0:n], in0=lnM[:, 0:n], scalar=-1.0,
                                                     in1=lnV[:, 0:n], op0=ALU.mult, op1=ALU.mult),
             reads=[lnM_b, lnV_b], writes=[lnM_b])
        for c in range(CC):
            ti = nxt("tmp", NTMP)
            S.op("dve", lambda v, c=c, ti=ti: v.tensor_tensor(out=tmpf[ti][:, 0:n], in0=convo[:, c, 0:n],
                                                              in1=lnV[:, 0:n], op=ALU.mult),
                 reads=[convo_b[c], lnV_b], writes=[tmpf_b[ti]])
            S.op("dve", lambda v, ti=ti: v.tensor_tensor(out=tmpf[ti][:, 0:n], in0=tmpf[ti][:, 0:n],
                                                         in1=lnM[:, 0:n], op=ALU.add),
                 reads=[tmpf_b[ti], lnM_b], writes=[tmpf_b[ti]])
            S.op("act", lambda a, c=c, ti=ti: a.activation(out=yb[:, c, 0:n], in_=tmpf[ti][:, 0:n], func=AF.Silu,
                                                           scale=vecs[:, V_LNG + c:V_LNG + c + 1],
                                                           bias=vecs[:, V_LNB + c:V_LNB + c + 1]),
                 reads=[tmpf_b[ti], vecs_b], writes=[yb_b[c]])

    def pw_stage(st, wpw, wpw_b, evac=True):
        n = st.n
        st.pwb = []
        for co in range(CC):
            bk, bk_b = bank(hold=True)
            mm_group(bk[:, 0:n], [(wpw[:, k, co * 128:(co + 1) * 128], yb[:, k, 0:n]) for k in range(CC)],
                     [wpw_b] + yb_b, bk_b)
            st.pwb.append((bk, bk_b))
        if evac:
            pw_evac(st)

    def pw_evac(st):
        n = st.n
        for co, (bk, bk_b) in enumerate(st.pwb):
            S.op("act", lambda a, bk=bk, co=co: a.activation(out=catT[:, co, 0:n], in_=bk[:, 0:n], func=AF.Copy),
                 reads=[bk_b], writes=[catT_b[co]])
            release(bk_b)

    def pool_dve(st, fixup, groups=(3, 2, 1, 0), eng="dve"):
        n, ui = st.n, st.ui
        L = 16 + n
        if not hasattr(st, "mix"):
            st.mix = []
        for g in groups:
            src, src_b = pbuf[ui][:, g, :], [pbuf_b[ui][g], phist_b[ui]]
            sh = 1
            for _lvl in range(g + 1):
                pi = nxt("psc", 2)
                S.op(eng, lambda v, src=src, sh=sh, pi=pi: v.tensor_tensor(
                    out=psc[pi][:, sh:L], in0=src[:, sh:L], in1=src[:, 0:L - sh], op=ALU.add),
                     reads=src_b, writes=[psc_b[pi]])
                src, src_b = psc[pi], [psc_b[pi]]
                sh *= 2
            mi = nxt("mixed", NMX)
            wdt = float(2 ** (g + 1))
            S.op("dve", lambda v, src=src, mi=mi, g=g, wdt=wdt: v.scalar_tensor_tensor(
                out=mixed[mi][:, 0:n], in0=src[:, 16:16 + n], scalar=1.0 / wdt, in1=pbuf[ui][:, g, 16:16 + n],
                op0=ALU.mult, op1=ALU.subtract),
                 reads=src_b + [pbuf_b[ui][g]], writes=[mixed_b[mi]])
            if fixup:
                S.op("dve", lambda v, src=src, g=g: v.tensor_tensor(out=fix, in0=src[:, 16:32], in1=rcnt[:, g, :],
                                                                    op=ALU.mult),
                     reads=src_b + [rcnt_b], writes=[fix_b])
                S.op("dve", lambda v, mi=mi, g=g: v.tensor_tensor(out=mixed[mi][:, 0:16], in0=fix,
                                                                  in1=pbuf[ui][:, g, 16:32], op=ALU.subtract),
                     reads=[fix_b, pbuf_b[ui][g]], writes=[mixed_b[mi]])
            st.mix.append((g, mi))

    def pool_pe(st, wpl, wpl_b):
        n = st.n
        for (g, mi) in st.mix:
            bk, bk_b = bank()
            mm_group(bk[:, 0:n], [(wpl[:, g, 0:128], mixed[mi][:, 0:n])], [wpl_b, mixed_b[mi]], bk_b)
            S.op("act", lambda a, bk=bk, g=g: a.activation(out=catT[:, CC + g, 0:n], in_=bk[:, 0:n], func=AF.Identity,
                                                           scale=vecs[:, V_PSC + g:V_PSC + g + 1]),
                 reads=[bk_b, vecs_b], writes=[catT_b[CC + g]])

    def wout_stage(st, wos):
        n, sub, sbufs = st.n, st.sub, st.sbufs
        for dc in range(KC):
            wo, wo_b = wos[dc // 4]
            dl = dc % 4
            bk, bk_b = bank()
            mm_group(bk[:, 0:n], [(wo[:, k, dl * 128:(dl + 1) * 128], catT[:, k, 0:n]) for k in range(2 * CC)],
                     [wo_b] + catT_b, bk_b)
            S.op("dve", lambda v, bk=bk, dc=dc: v.tensor_tensor(
                out=xT[:, dc, sub.sl], in0=bk[:, 0:n], in1=xT[:, dc, sub.sl], op=ALU.add),
                 reads=[bk_b, sbufs.x[dc]], writes=[sbufs.x[dc]])

    out_tokens = []

    def final_out(sub, sbufs):
        n = sub.n
        if debug_stop is not None:
            for c in range(KC):
                tok = S.dma("sp", lambda e, c=c: e.dma_start(out=outT_d[c * 128:(c + 1) * 128, sub.tok0:sub.tok0 + n],
                                                             in_=xT[:, c, sub.sl]),
                            reads=[sbufs.x[c]], group="o")
            return
        r, r_b = norm_stats(sub, sbufs)
        for c in range(KC):
            S.op("dve", lambda v, c=c: v.scalar_tensor_tensor(out=xT[:, c, sub.sl], in0=xT[:, c, sub.sl],
                                                              scalar=vecs[:, V_FIN + c:V_FIN + c + 1], in1=r[:, 0:n],
                                                              op0=ALU.mult, op1=ALU.mult),
                 reads=[sbufs.x[c], r_b, vecs_b], writes=[sbufs.x[c]])
            S.dma("sp", lambda e, c=c: e.dma_start(out=outT_d[c * 128:(c + 1) * 128, sub.tok0:sub.tok0 + n],
                                                   in_=xT[:, c, sub.sl]),
                  reads=[sbufs.x[c]], group="o")

    gsub = 0
    slotA = Sub("A", HALO, SUBW, 0, True)
    slotB = Sub("B", HALO + SUBW, SUBW, 0, True)
    slotH = Sub("H", 0, HALO, 0, False)
    bufsA, bufsB, bufsH = mk_sub_bufs(slotA), mk_sub_bufs(slotB), mk_sub_bufs(slotH)
    def load_x(sub, sbufs):
        for c in range(KC):
            d0 = (HALO + sub.tok0) if sub.main else 0
            S.dma("sp", lambda e, c=c, d0=d0: e.dma_start(
                out=xT[:, c, sub.sl], in_=xT_d[c * 128:(c + 1) * 128, d0:d0 + sub.n]),
                  writes=[sbufs.x[c]], group="x")

    for j in range(ntiles):
        A = Sub("A", HALO, SUBW, j * TT, True)
        B = Sub("B", HALO + SUBW, SUBW, j * TT + SUBW, True)
        subs = [(A, bufsA), (B, bufsB)]
        if j == 0:
            subs = [(slotH, bufsH)] + subs
            for (sub, sbufs) in subs:
                load_x(sub, sbufs)
            startup_gate.extend(bufsA.x)
            for (sub, sbufs) in subs:
                norm_to_h(sub, sbufs, V_FFN1)
        ffn("ffn1", subs, hook=lambda sub, sbufs: (lambda: norm_to_h(sub, sbufs, V_MIX)))
        msubs = []
        for (sub, sbufs) in subs:
            if sub.main:
                msubs.append((sub, sbufs, gsub % 2))
                gsub += 1
            else:
                msubs.append((sub, sbufs, 0))
        main = [(s_, b_) for (s_, b_) in subs if s_.main]

        def tail(sub, sbufs, j=j):
            final_out(sub, sbufs)
            if j + 1 < ntiles:
                nsub = Sub(sub.name, sub.c0, sub.n, sub.tok0 + TT, True)
                load_x(nsub, sbufs)
                return lambda: norm_to_h(nsub, sbufs, V_FFN1)
            return None

        if debug_stop is None:
            mixer(msubs, first_of_core=(j == 0), hook=lambda sub, sbufs: (lambda: norm_to_h_2(sub, sbufs, V_FFN2)))
            ffn("ffn2", main, hook=tail)
        else:
            mixer(msubs, first_of_core=(j == 0), hook=lambda sub, sbufs: None)
            for sp_ in wspecs_ffn("ffn2"):
                wtile(sp_[0])
            for (sub, sbufs) in main:
                tail(sub, sbufs)
    flush()
    assert wstate["cons"] == len(wlist)
    S.emit(final_group=("sp", "o"))
    return nc, S


_CACHE = {}


def _pack_vecs(ffn1_norm, mix_norm, ffn2_norm, final_norm, conv_dw_b, conv_ln_g, conv_ln_b, pool_scale, conv_dw):
    v = np.zeros((128, NV), np.float32)

    def put(col, a):
        a = np.asarray(a, np.float32)
        k = a.shape[0] // 128
        v[:, col:col + k] = a.reshape(k, 128).T
    put(V_FFN1, ffn1_norm)
    put(V_MIX, mix_norm)
    put(V_FFN2, ffn2_norm)
    put(V_FIN, final_norm)
    put(V_CB, conv_dw_b)
    put(V_LNG, conv_ln_g)
    put(V_LNB, conv_ln_b)
    put(V_PSC, pool_scale)
    dw = np.asarray(conv_dw, np.float32)
    dwp = np.zeros((2 * NPAIR, D_CONV), np.float32)
    dwp[:CONV_W] = dw
    for c in range(CC):
        for hh in range(2):
            ch = dwp[:, c * 128 + 64 * hh:c * 128 + 64 * hh + 64]
            ev, od = ch[0::2].T, ch[1::2].T
            col = V_DW + (c * 2 + hh) * NPAIR
            if hh == 0:
                v[0:64, col:col + NPAIR], v[64:128, col:col + NPAIR] = ev, od
            else:
                v[0:64, col:col + NPAIR], v[64:128, col:col + NPAIR] = od, ev
    return v


def kernel(x, ffn1_norm, ffn1_w_gate, ffn1_w_up, ffn1_w_down, mix_norm, w_in,
           conv_dw, conv_dw_b, conv_ln_g, conv_ln_b, conv_pw, pool_w, pool_scale,
           w_out, ffn2_norm, ffn2_w_gate, ffn2_w_up, ffn2_w_down, final_norm):
    x = np.asarray(x, np.float32)
    Bn, T, Dm = x.shape
    n_cores = 8
    per_b = n_cores // Bn
    TOK = T // per_b
    ntiles = TOK // TT
    key = ntiles
    if key not in _CACHE:
        _CACHE[key] = build_program(ntiles)[0]
    nc = _CACHE[key]
    f32c = lambda a: np.ascontiguousarray(np.asarray(a, np.float32))
    shared = {
        "ident": np.ascontiguousarray(np.tile(np.eye(64, dtype=np.float32), (2, 1))),
        "vecs": _pack_vecs(ffn1_norm, mix_norm, ffn2_norm, final_norm, conv_dw_b, conv_ln_g, conv_ln_b,
                           pool_scale, conv_dw),
        "ffn1_w_gate": f32c(ffn1_w_gate), "ffn1_w_up": f32c(ffn1_w_up), "ffn1_w_down": f32c(ffn1_w_down),
        "ffn2_w_gate": f32c(ffn2_w_gate), "ffn2_w_up": f32c(ffn2_w_up), "ffn2_w_down": f32c(ffn2_w_down),
        "w_in": f32c(w_in), "conv_pw": f32c(conv_pw), "pool_w": f32c(pool_w).reshape(4 * 128, 128),
        "w_out": f32c(w_out),
    }
    in_maps = []
    for i in range(n_cores):
        b, part = divmod(i, per_b)
        t0 = part * TOK
        xt = np.zeros((Dm, HALO + TOK), np.float32)
        xt[:, HALO:] = x[b, t0:t0 + TOK, :].T
        if part > 0:
            xt[:, :HALO] = x[b, t0 - HALO:t0, :].T
        pos = np.broadcast_to((t0 + np.arange(16, dtype=np.float32))[None, :], (128, 16))
        m = dict(shared)
        m["xT"] = xt
        m["pos"] = np.ascontiguousarray(pos, dtype=np.float32)
        in_maps.append(m)
    res = run_bass_kernel_spmd(nc, in_maps, core_ids=list(range(n_cores)))
    out = np.empty((Bn, T, Dm), np.float32)
    for i in range(n_cores):
        b, part = divmod(i, per_b)
        t0 = part * TOK
        out[b, t0:t0 + TOK, :] = np.asarray(res.results[i]["outT"], np.float32).T
    return out
```

```python
import numpy as np
import concourse.bass as bass
import concourse.mybir as mybir
from concourse.bass_utils import run_bass_kernel_spmd

F32 = mybir.dt.float32
BF16 = mybir.dt.bfloat16
AF = mybir.ActivationFunctionType
ALU = mybir.AluOpType

D_MODEL = 1024
D_FF = 2816
D_CONV = 512
CONV_W = 31
KC = 8
FCH = 11
CC = 4
HALO = 32
SUBW = 512
TT = 1024
NSLOT = 6
RSQ = "lnexp"
RMS_EPS = 1e-6
LN_EPS = 1e-5

V_FFN1, V_MIX, V_FFN2, V_FIN = 0, 8, 16, 24
V_CB, V_LNG, V_LNB, V_PSC = 32, 36, 40, 44
V_DW = 48
NPAIR = 16
NV = V_DW + CC * 2 * NPAIR

COMPUTE = ("pe", "act", "dve", "pool")


class Buf:
    __slots__ = ("name", "lastw", "readers", "psum")

    def __init__(self, name, psum=False):
        self.name = name
        self.lastw = None
        self.readers = []
        self.psum = psum


class Sched:
    def __init__(self, nc, n_dma_sems=8):
        self.nc = nc
        self.prog = {e: [] for e in COMPUTE + ("sp",)}
        self.sem = {e: nc.alloc_semaphore(f"s_{e}") for e in COMPUTE}
        self.cnt = {e: 0 for e in COMPUTE}
        self.waited = {e: {} for e in COMPUTE + ("sp",)}
        self.dma_sems = {}
        self.n_dma_sems = n_dma_sems
        self.dma_rr = {}

    def _need(self, eng, tok, waits):
        if tok is None:
            return
        key, sem, val = tok
        if self.waited[eng].get(key, 0) >= val:
            return
        self.waited[eng][key] = val
        waits[key] = (sem, val)

    def _deps(self, eng, reads, writes):
        waits = {}
        for b in reads:
            self._need(eng, b.lastw, waits)
            if b.psum:
                for t in b.readers:
                    if t[0] != eng:
                        self._need(eng, t, waits)
        for b in writes:
            self._need(eng, b.lastw, waits)
            for t in b.readers:
                self._need(eng, t, waits)
        return list(waits.values())

    def _commit(self, tok, reads, writes):
        for b in reads:
            b.readers.append(tok)
            if len(b.readers) > 12:
                best = {}
                for t in b.readers:
                    if t[0] not in best or best[t[0]][2] < t[2]:
                        best[t[0]] = t
                b.readers = list(best.values())
        for b in writes:
            b.lastw = tok
            b.readers = []

    def op(self, eng, fn, reads=(), writes=()):
        waits = self._deps(eng, reads, writes)
        self.cnt[eng] += 1
        tok = (eng, self.sem[eng], self.cnt[eng])
        self.prog[eng].append((waits, fn, self.sem[eng]))
        self._commit(tok, reads, writes)
        return tok

    def dma(self, eng, fn, reads=(), writes=(), group="g"):
        k = (eng, group)
        if k not in self.dma_sems:
            self.dma_sems[k] = [[self.nc.alloc_semaphore(f"d_{eng}_{group}_{i}"), 0]
                                for i in range(self.n_dma_sems)]
            self.dma_rr[k] = 0
        i = self.dma_rr[k]
        self.dma_rr[k] = (i + 1) % len(self.dma_sems[k])
        ent = self.dma_sems[k][i]
        sem, prev = ent
        waits = self._deps(eng, reads, writes)
        if prev > 0:
            w2 = {}
            self._need(eng, ((eng, group, i), sem, prev), w2)
            waits += list(w2.values())
        ent[1] = prev + 16
        tok = ((eng, group, i), sem, prev + 16)
        self.prog[eng].append((waits, fn, ("dma", sem)))
        self._commit(tok, reads, writes)
        return tok

    def emit(self, final_group=None):
        names = {"pe": "tensor", "act": "scalar", "dve": "vector", "pool": "gpsimd", "sp": "sync"}
        final_waits = []
        if final_group is not None:
            final_waits = [(None, sem, val) for (sem, val) in self.dma_sems[final_group] if val > 0]
        with self.nc.Block() as block:
            for e, attr in names.items():
                prog = self.prog[e]
                fw = final_waits if e == "sp" else ()

                def section(engine, prog=prog, fw=fw):
                    for waits, fn, sem in prog:
                        for (s, v) in waits:
                            engine.wait_ge(s, v)
                        ins = fn(engine)
                        if isinstance(sem, tuple):
                            ins.then_inc(sem[1], 16)
                        else:
                            ins.then_inc(sem, 1)
                    for (_k, s, v) in fw:
                        engine.wait_ge(s, v)

                getattr(block, attr)(section)


class Sub:
    def __init__(self, name, c0, n, tok0, main):
        self.name, self.c0, self.n, self.tok0, self.main = name, c0, n, tok0, main
        self.sl = slice(c0, c0 + n)


def build_program(ntiles=4, bcast_diag=True, debug_stop=None):
    TOK = ntiles * TT
    nc = bass.Bass("TRN2", target_bir_lowering=False)
    S = Sched(nc)

    def din(name, shape):
        return nc.dram_tensor(name, list(shape), F32, kind="ExternalInput").ap()

    xT_d = din("xT", [D_MODEL, HALO + TOK])
    pos_d = din("pos", [128, 16])
    vecs_d = din("vecs", [128, NV])
    ident_d = din("ident", [128, 64])
    W = {}
    for f in ("ffn1", "ffn2"):
        W[f + "_g"] = din(f + "_w_gate", [D_MODEL, D_FF])
        W[f + "_u"] = din(f + "_w_up", [D_MODEL, D_FF])
        W[f + "_d"] = din(f + "_w_down", [D_FF, D_MODEL])
    W["w_in"] = din("w_in", [D_MODEL, 3 * D_CONV])
    W["conv_pw"] = din("conv_pw", [D_CONV, D_CONV])
    W["pool_w"] = din("pool_w", [4 * 128, 128])
    W["w_out"] = din("w_out", [D_MODEL, D_MODEL])
    outT_d = nc.dram_tensor("outT", [D_MODEL, TOK], F32, kind="ExternalOutput").ap()

    WCOLS = HALO + TT
    sb = nc.alloc_sbuf_tensor
    xT = sb("xT_sb", [128, KC, WCOLS], F32)
    hT = sb("hT_sb", [128, KC, WCOLS], BF16)
    actT = sb("actT_sb", [128, FCH, WCOLS], BF16)
    wring = [sb(f"wr{i}", [128, 8, 512], BF16) for i in range(NSLOT)]
    wring_b = [Buf(f"wr{i}") for i in range(NSLOT)]
    NBF = 4
    bfr = [sb(f"bfr{i}", [128, SUBW], BF16) for i in range(NBF)]
    bfr_b = [Buf(f"bfr{i}") for i in range(NBF)]
    NTMP = 3
    tmpf = [sb(f"tmpf{i}", [128, SUBW], F32) for i in range(NTMP)]
    tmpf_b = [Buf(f"tmpf{i}") for i in range(NTMP)]
    NST = 2
    stt = [sb(f"stt{i}", [128, SUBW], F32) for i in range(NST)]
    stt_b = [Buf(f"stt{i}") for i in range(NST)]
    lnM = sb("lnM", [128, SUBW], F32)
    lnV = sb("lnV", [128, SUBW], F32)
    lnM_b, lnV_b = Buf("lnM"), Buf("lnV")
    ubuf = [sb(f"ubuf{i}", [128, CC, 2, HALO + SUBW], BF16) for i in range(2)]
    ubuf_b = [[Buf(f"u{i}_{c}") for c in range(CC)] for i in range(2)]
    urep_b = [[Buf(f"ur{i}_{c}") for c in range(CC)] for i in range(2)]
    uhist_b = [Buf(f"uh{i}") for i in range(2)]
    pbuf = [sb(f"pbuf{i}", [128, CC, 16 + SUBW], F32) for i in range(2)]
    pbuf_b = [[Buf(f"p{i}_{c}") for c in range(CC)] for i in range(2)]
    phist_b = [Buf(f"ph{i}") for i in range(2)]
    psc = [sb(f"psc{i}", [128, 16 + SUBW], F32) for i in range(2)]
    psc_b = [Buf(f"psc{i}") for i in range(2)]
    convo = sb("convo", [128, CC, SUBW], F32)
    convo_b = [Buf(f"convo{c}") for c in range(CC)]
    yb = sb("yb", [128, CC, SUBW], BF16)
    yb_b = [Buf(f"yb{c}") for c in range(CC)]
    NMX = 4
    mixed = [sb(f"mixed{i}", [128, SUBW], BF16) for i in range(NMX)]
    mixed_b = [Buf(f"mixed{i}") for i in range(NMX)]
    catT = sb("catT", [128, 2 * CC, SUBW], BF16)
    catT_b = [Buf(f"cat{c}") for c in range(2 * CC)]
    NDG = 3
    NDGD, NDGP = 2, 4
    dgr = [sb(f"dgr{i}", [128, 8, 64], BF16) for i in range(NDGD)]
    dgr_b = [Buf(f"dgr{i}") for i in range(NDGD)]
    dgp = [sb(f"dgp{i}", [128, 8, 64], BF16) for i in range(NDGP)]
    dgp_b = [Buf(f"dgp{i}") for i in range(NDGP)]
    vecs = sb("vecs_sb", [128, NV], F32)
    vecs_b = Buf("vecs")
    wh = sb("wh_sb", [128, CC * 2 * NPAIR], F32)
    wh_b = Buf("wh")
    ones = sb("ones_sb", [128, 128], BF16)
    ones_b = Buf("ones")
    identf = sb("identf_sb", [128, 64], F32)
    ident_b = Buf("ident")
    epsc = sb("epsc_sb", [128, 2], F32)
    negh_b = Buf("epsc")
    rcnt = sb("rcnt_sb", [128, CC, 16], F32)
    rcnt_b = Buf("rcnt")

    NBANK = 8
    banks = [nc.alloc_psum_tensor(f"bank{i}", [128, SUBW], F32) for i in range(NBANK)]
    banks_b = [Buf(f"bank{i}", psum=True) for i in range(NBANK)]
    rr = {"bank": 0, "bfr": 0, "tmp": 0, "stt": 0, "mixed": 0, "dgr": 0, "dgp": 0, "psc": 0}

    def nxt(kind, n):
        i = rr[kind]
        rr[kind] = (i + 1) % n
        return i

    held = set()

    def bank(hold=False):
        for _ in range(NBANK):
            i = nxt("bank", NBANK)
            if i not in held:
                break
        else:
            raise RuntimeError("all PSUM banks held")
        if hold:
            held.add(i)
        return banks[i], banks_b[i]

    def release(bk_b):
        held.discard(banks_b.index(bk_b))

    S.dma("sp", lambda e: e.dma_start(out=vecs[:], in_=vecs_d), writes=[vecs_b], group="c")
    posb, pos_b = lnV[:, 0:16], lnV_b
    fix, fix_b = lnM[:, 0:16], lnM_b
    S.dma("sp", lambda e: e.dma_start(out=posb, in_=pos_d), writes=[pos_b], group="c")
    S.op("dve", lambda v: v.memset(ones[:], 1.0), writes=[ones_b])
    S.op("dve", lambda v: v.memset(epsc[:, 0:1], RMS_EPS), writes=[negh_b])
    S.op("dve", lambda v: v.memset(epsc[:, 1:2], LN_EPS), reads=[negh_b], writes=[negh_b])
    S.dma("sp", lambda e: e.dma_start(out=identf[:], in_=ident_d), writes=[ident_b], group="c")
    for i in range(2):
        S.op("dve", lambda v, i=i: v.memset(psc[i][:], 0.0), writes=[psc_b[i]])
        S.op("pool", lambda g, i=i: g.memset(ubuf[i][:], 0.0),
             writes=ubuf_b[i] + urep_b[i] + [uhist_b[i]])
    S.op("dve", lambda v: v.tensor_scalar(out=wh[:], in0=vecs[:, V_DW:V_DW + CC * 2 * NPAIR], scalar1=0.5,
                                          scalar2=None, op0=ALU.mult), reads=[vecs_b], writes=[wh_b])
    for g in range(CC):
        S.op("dve", lambda v, g=g: v.tensor_scalar(out=rcnt[:, g, :], in0=posb, scalar1=1.0,
                                                   scalar2=float(2 ** (g + 1)), op0=ALU.add, op1=ALU.min),
             reads=[pos_b], writes=[rcnt_b])
    S.op("dve", lambda v: v.reciprocal(out=rcnt[:], in_=rcnt[:]), reads=[rcnt_b], writes=[rcnt_b])

    def wspecs_ffn(f):
        specs = []
        for half in range(2):
            f0 = half * FCH
            groups = [(g0, min(4, FCH - g0)) for g0 in range(0, FCH, 4)]
            for q, (g0, nf) in enumerate(groups):
                if half == 0 and q < 2:
                    rels = (4 - 2 * q, 3 - 2 * q)
                else:
                    rels = (2, 1)
                for nm, rel in zip(("_g", "_u"), rels):
                    specs.append(((f + nm, 0, 8, (f0 + g0) * 128, nf * 128), rel))
            for dg in range(2):
                for k0, r in ((0, 2), (8, 1)):
                    nk = min(8, FCH - k0)
                    rel = r if half == 0 else (4 - 2 * dg - (0 if k0 == 0 else 1))
                    specs.append(((f + "_d", (f0 + k0) * 128, nk, dg * 512, 512), rel))
        return specs

    def wspecs_mixer():
        return [(("w_in", 0, 8, 0, 512), 3), (("w_in", 0, 8, 512, 512), 2), (("w_in", 0, 8, 1024, 512), 1),
                (("conv_pw", 0, 4, 0, 512), 4), (("pool_w", 0, 4, 0, 128), 3),
                (("w_out", 0, 8, 0, 512), 2), (("w_out", 0, 8, 512, 512), 1)]

    wlist = []
    for _j in range(ntiles):
        wlist += wspecs_ffn("ffn1") + wspecs_mixer() + wspecs_ffn("ffn2")
    wstate = {"loaded": 0, "cons": 0}
    startup_gate = []

    def w_issue(i):
        name, r0, nk, c0, ncols = wlist[i][0]
        slot = i % NSLOT
        src = W[name][r0:r0 + nk * 128, c0:c0 + ncols].rearrange("(k p) c -> p k c", p=128)
        dst = wring[slot][:, 0:nk, 0:ncols]
        gate = list(startup_gate) if 4 <= i < NSLOT else []
        S.dma("pool", lambda e, dst=dst, src=src: e.dma_start(out=dst, in_=src),
              reads=gate, writes=[wring_b[slot]], group="w")

    def wtile(spec):
        i = wstate["cons"]
        assert wlist[i][0] == spec, (i, wlist[i], spec)
        while wstate["loaded"] < min(len(wlist), i + NSLOT):
            m = wstate["loaded"]
            if m >= NSLOT:
                k = m - NSLOT
                if k + wlist[k][1] > i:
                    break
            w_issue(m)
            wstate["loaded"] += 1
        assert wstate["loaded"] > i
        wstate["cons"] += 1
        return wring[i % NSLOT], wring_b[i % NSLOT]

    class SB:
        pass

    def mk_sub_bufs(sub):
        o = SB()
        o.x = [Buf(f"x{c}{sub.name}") for c in range(KC)]
        o.h = [Buf(f"h{c}{sub.name}") for c in range(KC)]
        o.a = [Buf(f"a{c}{sub.name}") for c in range(FCH)]
        return o

    def mm_group(ps_ap, pairs, reads, bank_b):
        def fn(t, pairs=pairs, ps_ap=ps_ap):
            ins = None
            n = len(pairs)
            for i, (l, r) in enumerate(pairs):
                ins = t.matmul(ps_ap, lhsT=l, rhs=r, start=(i == 0), stop=(i == n - 1))
            return ins
        S.op("pe", fn, reads=reads, writes=[bank_b])

    def norm_sums(sub, sbufs):
        n = sub.n
        bk, bk_b = bank(hold=True)
        for c in range(KC):
            i = nxt("bfr", NBF)
            S.op("act", lambda a, i=i, c=c: a.activation(out=bfr[i][:, 0:n], in_=xT[:, c, sub.sl], func=AF.Square),
                 reads=[sbufs.x[c]], writes=[bfr_b[i]])
            S.op("pe", lambda t, i=i, c=c: t.matmul(bk[:, 0:n], lhsT=ones[:], rhs=bfr[i][:, 0:n],
                                                    start=(c == 0), stop=(c == KC - 1)),
                 reads=[bfr_b[i], ones_b], writes=[bk_b])
        return bk, bk_b

    def norm_rstd(sub, bk, bk_b):
        n = sub.n
        si = nxt("stt", NST)
        S.op("act", lambda a: a.activation(out=stt[si][:, 0:n], in_=bk[:, 0:n], func=AF.Ln,
                                           scale=1.0 / D_MODEL, bias=epsc[:, 0:1]),
             reads=[bk_b, negh_b], writes=[stt_b[si]])
        release(bk_b)
        S.op("act", lambda a: a.activation(out=stt[si][:, 0:n], in_=stt[si][:, 0:n], func=AF.Exp, scale=-0.5),
             reads=[stt_b[si]], writes=[stt_b[si]])
        return stt[si], stt_b[si]

    def norm_stats(sub, sbufs):
        bk, bk_b = norm_sums(sub, sbufs)
        return norm_rstd(sub, bk, bk_b)

    def h_ops(sub, sbufs, vcol, r, r_b):
        n = sub.n
        for c in range(KC):
            S.op("dve", lambda v, c=c: v.scalar_tensor_tensor(out=hT[:, c, sub.sl], in0=xT[:, c, sub.sl],
                                                              scalar=vecs[:, vcol + c:vcol + c + 1], in1=r[:, 0:n],
                                                              op0=ALU.mult, op1=ALU.mult),
                 reads=[sbufs.x[c], r_b, vecs_b], writes=[sbufs.h[c]])

    def norm_to_h_2(sub, sbufs, vcol):
        bk, bk_b = norm_sums(sub, sbufs)

        def part2():
            r, r_b = norm_rstd(sub, bk, bk_b)
            h_ops(sub, sbufs, vcol, r, r_b)
        return part2

    def norm_to_h(sub, sbufs, vcol):
        r, r_b = norm_stats(sub, sbufs)
        h_ops(sub, sbufs, vcol, r, r_b)

    pending = []

    def flush():
        while pending:
            r_ = pending.pop(0)()
            if callable(r_):
                r_()

    def ffn(f, subs, hook=None):
        def up_chunk(sub, sbufs, wg, wg_b, wu, wu_b, fi, fl):
            n = sub.n
            bg, bg_b = bank()
            mm_group(bg[:, 0:n], [(wg[:, k, fi * 128:(fi + 1) * 128], hT[:, k, sub.sl]) for k in range(KC)],
                     [wg_b] + sbufs.h, bg_b)
            bu, bu_b = bank()
            mm_group(bu[:, 0:n], [(wu[:, k, fi * 128:(fi + 1) * 128], hT[:, k, sub.sl]) for k in range(KC)],
                     [wu_b] + sbufs.h, bu_b)
            ti = nxt("tmp", NTMP)
            S.op("act", lambda a: a.activation(out=tmpf[ti][:, 0:n], in_=bg[:, 0:n], func=AF.Silu),
                 reads=[bg_b], writes=[tmpf_b[ti]])
            S.op("dve", lambda v: v.tensor_tensor(out=actT[:, fl, sub.sl], in0=bu[:, 0:n], in1=tmpf[ti][:, 0:n],
                                                  op=ALU.mult),
                 reads=[bu_b, tmpf_b[ti]], writes=[sbufs.a[fl]])

        def down_chunk(sub, sbufs, wd0, wd0_b, wd1, wd1_b, dg, dl):
            n = sub.n
            dc = dg * 4 + dl
            bk, bk_b = bank()
            pairs = []
            for fl in range(FCH):
                wt = wd0 if fl < 8 else wd1
                pairs.append((wt[:, fl % 8, dl * 128:(dl + 1) * 128], actT[:, fl, sub.sl]))
            mm_group(bk[:, 0:n], pairs, [wd0_b, wd1_b] + sbufs.a, bk_b)
            S.op("dve", lambda v: v.scalar_tensor_tensor(out=xT[:, dc, sub.sl], in0=bk[:, 0:n], scalar=0.5,
                                                         in1=xT[:, dc, sub.sl], op0=ALU.mult, op1=ALU.add),
                 reads=[bk_b, sbufs.x[dc]], writes=[sbufs.x[dc]])

        for half in range(2):
            f0 = half * FCH
            groups = []
            for g0 in range(0, FCH, 4):
                nf = min(4, FCH - g0)
                groups.append((g0, nf))
            gi = 0
            if half == 0:
                tl = []
                for (g0, nf) in groups[:2]:
                    tl.append(wtile((f + "_g", 0, 8, (f0 + g0) * 128, nf * 128)))
                    tl.append(wtile((f + "_u", 0, 8, (f0 + g0) * 128, nf * 128)))
                for (sub, sbufs) in subs:
                    for q, (g0, nf) in enumerate(groups[:2]):
                        (wg, wg_b), (wu, wu_b) = tl[2 * q], tl[2 * q + 1]
                        for fi in range(nf):
                            up_chunk(sub, sbufs, wg, wg_b, wu, wu_b, fi, g0 + fi)
                            if sub.main and q == 1 and fi == 1:
                                flush()
                gi = 2
            for (g0, nf) in groups[gi:]:
                wg, wg_b = wtile((f + "_g", 0, 8, (f0 + g0) * 128, nf * 128))
                wu, wu_b = wtile((f + "_u", 0, 8, (f0 + g0) * 128, nf * 128))
                for (sub, sbufs) in subs:
                    for fi in range(nf):
                        up_chunk(sub, sbufs, wg, wg_b, wu, wu_b, fi, g0 + fi)
            if half == 0:
                for dg in range(2):
                    wd0, wd0_b = wtile((f + "_d", (f0 + 0) * 128, 8, dg * 512, 512))
                    wd1, wd1_b = wtile((f + "_d", (f0 + 8) * 128, FCH - 8, dg * 512, 512))
                    for (sub, sbufs) in subs:
                        for dl in range(4):
                            down_chunk(sub, sbufs, wd0, wd0_b, wd1, wd1_b, dg, dl)
            else:
                tl = []
                for dg in range(2):
                    tl.append(wtile((f + "_d", (f0 + 0) * 128, 8, dg * 512, 512)))
                    tl.append(wtile((f + "_d", (f0 + 8) * 128, FCH - 8, dg * 512, 512)))
                for (sub, sbufs) in subs:
                    for dg in range(2):
                        (wd0, wd0_b), (wd1, wd1_b) = tl[2 * dg], tl[2 * dg + 1]
                        for dl in range(4):
                            down_chunk(sub, sbufs, wd0, wd0_b, wd1, wd1_b, dg, dl)
                            if dg == 1 and dl == 1 and sub.main:
                                flush()
                    if hook is not None:
                        p2 = hook(sub, sbufs)
                        if p2 is not None:
                            pending.append(p2)

    def mixer(subs, first_of_core, hook):
        wa, wa_b = wtile(("w_in", 0, 8, 0, 512))
        wg, wg_b = wtile(("w_in", 0, 8, 512, 512))
        wp, wp_b = wtile(("w_in", 0, 8, 1024, 512))
        first_main = [True]
        deferred = []

        def hist_copy(ui):
            nu = 1 - ui
            S.op("pool", lambda g: g.tensor_copy(out=ubuf[nu][0:64, :, 0, 0:HALO],
                                                 in_=ubuf[ui][0:64, :, 0, SUBW:SUBW + HALO]),
                 reads=ubuf_b[ui], writes=[uhist_b[nu]])
            S.op("pool", lambda g: g.tensor_copy(out=ubuf[nu][64:128, :, 1, 0:HALO],
                                                 in_=ubuf[ui][64:128, :, 1, SUBW:SUBW + HALO]),
                 reads=ubuf_b[ui], writes=[uhist_b[nu]])
            S.op("pool", lambda g: g.tensor_copy(out=pbuf[nu][:, :, 0:16], in_=pbuf[ui][:, :, SUBW:SUBW + 16]),
                 reads=pbuf_b[ui], writes=[phist_b[nu]])

        msubs = [s_ for s_ in subs if s_[0].main]
        sts = [MixState(sub, sbufs, ui) for (sub, sbufs, ui) in msubs]
        A_, B_ = sts
        for (sub, sbufs, ui) in subs:
            n = sub.n
            ucol = HALO if sub.main else 0
            for c in range(CC):
                ba, ba_b = bank()
                mm_group(ba[:, 0:n], [(wa[:, k, c * 128:(c + 1) * 128], hT[:, k, sub.sl]) for k in range(KC)],
                         [wa_b] + sbufs.h, ba_b)
                bg, bg_b = bank()
                mm_group(bg[:, 0:n], [(wg[:, k, c * 128:(c + 1) * 128], hT[:, k, sub.sl]) for k in range(KC)],
                         [wg_b] + sbufs.h, bg_b)
                ti = nxt("tmp", NTMP)
                S.op("act", lambda a, ti=ti, bg=bg, n=n: a.activation(out=tmpf[ti][:, 0:n], in_=bg[:, 0:n],
                                                                      func=AF.Tanh, scale=0.5),
                     reads=[bg_b], writes=[tmpf_b[ti]])
                uw = [ubuf_b[ui][c]] if sub.main else [uhist_b[ui]]
                for hh in range(2):
                    ps_ = slice(64 * hh, 64 * hh + 64)
                    S.op("dve", lambda v, ti=ti, ba=ba, n=n, c=c, ui=ui, ucol=ucol, hh=hh, ps_=ps_: v.scalar_tensor_tensor(
                        out=ubuf[ui][ps_, c, hh, ucol:ucol + n], in0=tmpf[ti][ps_, 0:n], scalar=1.0, in1=ba[ps_, 0:n],
                        op0=ALU.add, op1=ALU.mult),
                         reads=[ba_b, tmpf_b[ti]], writes=uw)
                if sub.main:
                    WU = HALO + SUBW
                    S.dma("sp", lambda e, c=c, ui=ui: e.dma_start(out=ubuf[ui][64:128, c, 0, 0:WU - 1],
                                                                  in_=ubuf[ui][0:64, c, 0, 1:WU]),
                          reads=[ubuf_b[ui][c], uhist_b[ui]], writes=[urep_b[ui][c]], group="r")
                    S.dma("sp", lambda e, c=c, ui=ui: e.dma_start(out=ubuf[ui][0:64, c, 1, 0:WU - 1],
                                                                  in_=ubuf[ui][64:128, c, 1, 1:WU]),
                          reads=[ubuf_b[ui][c], uhist_b[ui], urep_b[ui][c]], writes=[urep_b[ui][c]], group="r")
                if sub.main and first_main[0] and c == 1:
                    flush()
            for c in range(CC):
                bp, bp_b = bank()
                mm_group(bp[:, 0:n], [(wp[:, k, c * 128:(c + 1) * 128], hT[:, k, sub.sl]) for k in range(KC)],
                         [wp_b] + sbufs.h, bp_b)
                if sub.main:
                    S.op("act", lambda a, bp=bp, c=c, ui=ui, n=n: a.activation(
                        out=pbuf[ui][:, c, 16:16 + n], in_=bp[:, 0:n], func=AF.Identity),
                         reads=[bp_b], writes=[pbuf_b[ui][c]])
                else:
                    S.op("act", lambda a, bp=bp, c=c, ui=ui: a.activation(
                        out=pbuf[ui][:, c, 0:16], in_=bp[:, 16:32], func=AF.Identity),
                         reads=[bp_b], writes=[phist_b[ui]])
            if sub.main:
                if first_main[0]:
                    hist_copy(ui)
                    first_main[0] = False
                    pool_dve(A_, first_of_core, eng="pool")
                else:
                    deferred.append(ui)
        wpw, wpw_b = wtile(("conv_pw", 0, 4, 0, 512))
        wpl, wpl_b = wtile(("pool_w", 0, 4, 0, 128))
        wos = [wtile(("w_out", 0, 8, 0, 512)), wtile(("w_out", 0, 8, 512, 512))]
        for c in range(CC):
            conv_chunk(A_, c)
        B_.all_pool = True
        conv_chunk(B_, 0)
        evac_part(A_)
        pool_pe(A_, wpl, wpl_b)
        ln_apply(A_)
        conv_chunk(B_, 1)
        conv_chunk(B_, 2)
        conv_chunk(B_, 3)
        pool_dve(B_, False, eng="pool")
        pw_stage(A_, wpw, wpw_b, evac=False)
        evac_part(B_)
        pw_evac(A_)
        ln_apply(B_)
        wout_stage(A_, wos)
        p2 = hook(A_.sub, A_.sbufs)
        while deferred:
            hist_copy(deferred.pop(0))
        pw_stage(B_, wpw, wpw_b)
        p2b = p2() if p2 is not None else None
        pool_pe(B_, wpl, wpl_b)
        if callable(p2b):
            p2b()
        wout_stage(B_, wos)
        p2 = hook(B_.sub, B_.sbufs)
        if p2 is not None:
            pending.append(p2)

    class MixState:
        def __init__(self, sub, sbufs, ui):
            self.sub, self.sbufs, self.ui, self.n = sub, sbufs, ui, sub.n
            self.cbanks = [None] * CC
            self.done_c = 0
            self.all_pool = False

    def conv_chunk(st, c):
        n, ui = st.n, st.ui
        bk, bk_b = bank(hold=True)
        st.cbanks[c] = (bk, bk_b)
        for r in range(2):
            tl = []
            for hh in range(2):
                on_pool = (st.all_pool and c >= 1) or (hh == 1)
                if on_pool:
                    di = nxt("dgp", NDGP)
                    dt_, dt_b = dgp[di], dgp_b[di]
                else:
                    di = nxt("dgr", NDGD)
                    dt_, dt_b = dgr[di], dgr_b[di]
                base = (c * 2 + hh) * NPAIR + r * 8
                wsl = wh[:, base:base + 8]
                S.op("pool" if on_pool else "dve", lambda v, dt_=dt_, wsl=wsl: v.tensor_tensor(
                    out=dt_[:], in0=identf[:].unsqueeze(1).to_broadcast([128, 8, 64]),
                    in1=wsl.unsqueeze(2).to_broadcast([128, 8, 64]), op=ALU.mult),
                     reads=[ident_b, wh_b], writes=[dt_b])
                tl.append((dt_, dt_b))

            def fn(t, tl=tl, r=r):
                ins = None
                for jl in range(8):
                    jj = r * 8 + jl
                    for hh in range(2):
                        ins = t.matmul(bk[64 * hh:64 * hh + 64, 0:n], lhsT=tl[hh][0][:, jl, :],
                                       rhs=ubuf[ui][:, c, hh, 2 + 2 * jj:2 + 2 * jj + n],
                                       start=(jj == 0), stop=(jj == NPAIR - 1), tile_position=(0, 64 * hh))
                return ins
            S.op("pe", fn, reads=[tl[0][1], tl[1][1], ubuf_b[ui][c], urep_b[ui][c], uhist_b[ui]], writes=[bk_b])

    def evac_part(st):
        n = st.n
        st.s1 = None
        st.bfi = []
        for c in range(CC):
            bk, bk_b = st.cbanks[c]
            bcol = vecs[:, V_CB + c:V_CB + c + 1]
            i1 = nxt("bfr", NBF)
            S.op("act", lambda a, bk=bk, i1=i1, bcol=bcol: a.activation(out=bfr[i1][:, 0:n], in_=bk[:, 0:n],
                                                                        func=AF.Identity, bias=bcol),
                 reads=[bk_b, vecs_b], writes=[bfr_b[i1]])
            i2 = nxt("bfr", NBF)
            S.op("act", lambda a, bk=bk, i2=i2, bcol=bcol: a.activation(out=bfr[i2][:, 0:n], in_=bk[:, 0:n],
                                                                        func=AF.Square, bias=bcol),
                 reads=[bk_b, vecs_b], writes=[bfr_b[i2]])
            S.op("dve", lambda v, bk=bk, c=c, bcol=bcol: v.tensor_scalar(out=convo[:, c, 0:n], in0=bk[:, 0:n],
                                                                         scalar1=bcol, scalar2=None, op0=ALU.add),
                 reads=[bk_b, vecs_b, bfr_b[i1], bfr_b[i2]], writes=[convo_b[c]])
            release(bk_b)
            st.bfi.append((i1, i2))
            if len(st.bfi) == 2 or c == CC - 1:
                if st.s1 is None:
                    st.s1, st.s1_b = bank(hold=True)
                    st.s2, st.s2_b = bank(hold=True)
                stats_pe(st)

    def stats_pe(st):
        n = st.n
        s1, s1_b, s2, s2_b = st.s1, st.s1_b, st.s2, st.s2_b
        while st.bfi:
            c = st.done_c
            i1, i2 = st.bfi.pop(0)
            S.op("pe", lambda t, i1=i1, c=c: t.matmul(s1[:, 0:n], lhsT=ones[:], rhs=bfr[i1][:, 0:n],
                                                      start=(c == 0), stop=(c == CC - 1)),
                 reads=[bfr_b[i1], ones_b], writes=[s1_b])
            S.op("pe", lambda t, i2=i2, c=c: t.matmul(s2[:, 0:n], lhsT=ones[:], rhs=bfr[i2][:, 0:n],
                                                      start=(c == 0), stop=(c == CC - 1)),
                 reads=[bfr_b[i2], ones_b], writes=[s2_b])
            st.done_c += 1

    def ln_apply(st):
        n = st.n
        s1, s1_b, s2, s2_b = st.s1, st.s1_b, st.s2, st.s2_b
        S.op("dve", lambda v: v.tensor_scalar(out=lnM[:, 0:n], in0=s1[:, 0:n], scalar1=1.0 / D_CONV, scalar2=None,
                                              op0=ALU.mult), reads=[s1_b], writes=[lnM_b])
        S.op("dve", lambda v: v.tensor_tensor(out=lnV[:, 0:n], in0=lnM[:, 0:n], in1=lnM[:, 0:n], op=ALU.mult),
             reads=[lnM_b], writes=[lnV_b])
        S.op("dve", lambda v: v.scalar_tensor_tensor(out=lnV[:, 0:n], in0=s2[:, 0:n], scalar=1.0 / D_CONV,
                                                     in1=lnV[:, 0:n], op0=ALU.mult, op1=ALU.subtract),
             reads=[s2_b, lnV_b], writes=[lnV_b])
        release(s1_b)
        release(s2_b)
        if RSQ == "lnexp":
            S.op("act", lambda a: a.activation(out=lnV[:, 0:n], in_=lnV[:, 0:n], func=AF.Ln, bias=epsc[:, 1:2]),
                 reads=[lnV_b, negh_b], writes=[lnV_b])
            S.op("act", lambda a: a.activation(out=lnV[:, 0:n], in_=lnV[:, 0:n], func=AF.Exp, scale=-0.5),
                 reads=[lnV_b], writes=[lnV_b])
        else:
            S.op("act", lambda a: a.activation(out=lnV[:, 0:n], in_=lnV[:, 0:n], func=AF.Sqrt, bias=epsc[:, 1:2]),
                 reads=[lnV_b, negh_b], writes=[lnV_b])
            S.op("dve", lambda v: v.reciprocal(out=lnV[:, 0:n], in_=lnV[:, 0:n]), reads=[lnV_b], writes=[lnV_b])
        S.op("dve", lambda v: v.scalar_tensor_tensor(out=lnM[:, 0:n], in0=lnM[:, 0:n], scalar=-1.0,
                                                     in1=lnV[:, 0:n], op0=ALU.mult, op1=ALU.mult),
             reads=[lnM_b, lnV_b], writes=[lnM_b])
        for c in range(CC):
            ti = nxt("tmp", NTMP)
            S.op("dve", lambda v, c=c, ti=ti: v.tensor_tensor(out=tmpf[ti][:, 0:n], in0=convo[:, c, 0:n],
                                                              in1=lnV[:, 0:n], op=ALU.mult),
                 reads=[convo_b[c], lnV_b], writes=[tmpf_b[ti]])
            S.op("dve", lambda v, ti=ti: v.tensor_tensor(out=tmpf[ti][:, 0:n], in0=tmpf[ti][:, 0:n],
                                                         in1=lnM[:, 0:n], op=ALU.add),
                 reads=[tmpf_b[ti], lnM_b], writes=[tmpf_b[ti]])
            S.op("act", lambda a, c=c, ti=ti: a.activation(out=yb[:, c, 0:n], in_=tmpf[ti][:, 0:n], func=AF.Silu,
                                                           scale=vecs[:, V_LNG + c:V_LNG + c + 1],
                                                           bias=vecs[:, V_LNB + c:V_LNB + c + 1]),
                 reads=[tmpf_b[ti], vecs_b], writes=[yb_b[c]])

    def pw_stage(st, wpw, wpw_b, evac=True):
        n = st.n
        st.pwb = []
        for co in range(CC):
            bk, bk_b = bank(hold=True)
            mm_group(bk[:, 0:n], [(wpw[:, k, co * 128:(co + 1) * 128], yb[:, k, 0:n]) for k in range(CC)],
                     [wpw_b] + yb_b, bk_b)
            st.pwb.append((bk, bk_b))
        if evac:
            pw_evac(st)

    def pw_evac(st):
        n = st.n
        for co, (bk, bk_b) in enumerate(st.pwb):
            S.op("act", lambda a, bk=bk, co=co: a.activation(out=catT[:, co, 0:n], in_=bk[:, 0:n], func=AF.Copy),
                 reads=[bk_b], writes=[catT_b[co]])
            release(bk_b)

    def pool_dve(st, fixup, groups=(3, 2, 1, 0), eng="dve"):
        n, ui = st.n, st.ui
        L = 16 + n
        if not hasattr(st, "mix"):
            st.mix = []
        for g in groups:
            src, src_b = pbuf[ui][:, g, :], [pbuf_b[ui][g], phist_b[ui]]
            sh = 1
            for _lvl in range(g + 1):
                pi = nxt("psc", 2)
                S.op(eng, lambda v, src=src, sh=sh, pi=pi: v.tensor_tensor(
                    out=psc[pi][:, sh:L], in0=src[:, sh:L], in1=src[:, 0:L - sh], op=ALU.add),
                     reads=src_b, writes=[psc_b[pi]])
                src, src_b = psc[pi], [psc_b[pi]]
                sh *= 2
            mi = nxt("mixed", NMX)
            wdt = float(2 ** (g + 1))
            S.op("dve", lambda v, src=src, mi=mi, g=g, wdt=wdt: v.scalar_tensor_tensor(
                out=mixed[mi][:, 0:n], in0=src[:, 16:16 + n], scalar=1.0 / wdt, in1=pbuf[ui][:, g, 16:16 + n],
                op0=ALU.mult, op1=ALU.subtract),
                 reads=src_b + [pbuf_b[ui][g]], writes=[mixed_b[mi]])
            if fixup:
                S.op("dve", lambda v, src=src, g=g: v.tensor_tensor(out=fix, in0=src[:, 16:32], in1=rcnt[:, g, :],
                                                                    op=ALU.mult),
                     reads=src_b + [rcnt_b], writes=[fix_b])
                S.op("dve", lambda v, mi=mi, g=g: v.tensor_tensor(out=mixed[mi][:, 0:16], in0=fix,
                                                                  in1=pbuf[ui][:, g, 16:32], op=ALU.subtract),
                     reads=[fix_b, pbuf_b[ui][g]], writes=[mixed_b[mi]])
            st.mix.append((g, mi))

    def pool_pe(st, wpl, wpl_b):
        n = st.n
        for (g, mi) in st.mix:
            bk, bk_b = bank()
            mm_group(bk[:, 0:n], [(wpl[:, g, 0:128], mixed[mi][:, 0:n])], [wpl_b, mixed_b[mi]], bk_b)
            S.op("act", lambda a, bk=bk, g=g: a.activation(out=catT[:, CC + g, 0:n], in_=bk[:, 0:n], func=AF.Identity,
                                                           scale=vecs[:, V_PSC + g:V_PSC + g + 1]),
                 reads=[bk_b, vecs_b], writes=[catT_b[CC + g]])

    def wout_stage(st, wos):
        n, sub, sbufs = st.n, st.sub, st.sbufs
        for dc in range(KC):
            wo, wo_b = wos[dc // 4]
            dl = dc % 4
            bk, bk_b = bank()
            mm_group(bk[:, 0:n], [(wo[:, k, dl * 128:(dl + 1) * 128], catT[:, k, 0:n]) for k in range(2 * CC)],
                     [wo_b] + catT_b, bk_b)
            S.op("dve", lambda v, bk=bk, dc=dc: v.tensor_tensor(
                out=xT[:, dc, sub.sl], in0=bk[:, 0:n], in1=xT[:, dc, sub.sl], op=ALU.add),
                 reads=[bk_b, sbufs.x[dc]], writes=[sbufs.x[dc]])

    out_tokens = []

    def final_out(sub, sbufs):
        n = sub.n
        if debug_stop is not None:
            for c in range(KC):
                tok = S.dma("sp", lambda e, c=c: e.dma_start(out=outT_d[c * 128:(c + 1) * 128, sub.tok0:sub.tok0 + n],
                                                             in_=xT[:, c, sub.sl]),
                            reads=[sbufs.x[c]], group="o")
            return
        r, r_b = norm_stats(sub, sbufs)
        for c in range(KC):
            S.op("dve", lambda v, c=c: v.scalar_tensor_tensor(out=xT[:, c, sub.sl], in0=xT[:, c, sub.sl],
                                                              scalar=vecs[:, V_FIN + c:V_FIN + c + 1], in1=r[:, 0:n],
                                                              op0=ALU.mult, op1=ALU.mult),
                 reads=[sbufs.x[c], r_b, vecs_b], writes=[sbufs.x[c]])
            S.dma("sp", lambda e, c=c: e.dma_start(out=outT_d[c * 128:(c + 1) * 128, sub.tok0:sub.tok0 + n],
                                                   in_=xT[:, c, sub.sl]),
                  reads=[sbufs.x[c]], group="o")

    gsub = 0
    slotA = Sub("A", HALO, SUBW, 0, True)
    slotB = Sub("B", HALO + SUBW, SUBW, 0, True)
    slotH = Sub("H", 0, HALO, 0, False)
    bufsA, bufsB, bufsH = mk_sub_bufs(slotA), mk_sub_bufs(slotB), mk_sub_bufs(slotH)
    def load_x(sub, sbufs):
        for c in range(KC):
            d0 = (HALO + sub.tok0) if sub.main else 0
            S.dma("sp", lambda e, c=c, d0=d0: e.dma_start(
                out=xT[:, c, sub.sl], in_=xT_d[c * 128:(c + 1) * 128, d0:d0 + sub.n]),
                  writes=[sbufs.x[c]], group="x")

    for j in range(ntiles):
        A = Sub("A", HALO, SUBW, j * TT, True)
        B = Sub("B", HALO + SUBW, SUBW, j * TT + SUBW, True)
        subs = [(A, bufsA), (B, bufsB)]
        if j == 0:
            subs = [(slotH, bufsH)] + subs
            for (sub, sbufs) in subs:
                load_x(sub, sbufs)
            startup_gate.extend(bufsA.x)
            for (sub, sbufs) in subs:
                norm_to_h(sub, sbufs, V_FFN1)
        ffn("ffn1", subs, hook=lambda sub, sbufs: (lambda: norm_to_h(sub, sbufs, V_MIX)))
        msubs = []
        for (sub, sbufs) in subs:
            if sub.main:
                msubs.append((sub, sbufs, gsub % 2))
                gsub += 1
            else:
                msubs.append((sub, sbufs, 0))
        main = [(s_, b_) for (s_, b_) in subs if s_.main]

        def tail(sub, sbufs, j=j):
            final_out(sub, sbufs)
            if j + 1 < ntiles:
                nsub = Sub(sub.name, sub.c0, sub.n, sub.tok0 + TT, True)
                load_x(nsub, sbufs)
                return lambda: norm_to_h(nsub, sbufs, V_FFN1)
            return None

        if debug_stop is None:
            mixer(msubs, first_of_core=(j == 0), hook=lambda sub, sbufs: (lambda: norm_to_h_2(sub, sbufs, V_FFN2)))
            ffn("ffn2", main, hook=tail)
        else:
            mixer(msubs, first_of_core=(j == 0), hook=lambda sub, sbufs: None)
            for sp_ in wspecs_ffn("ffn2"):
                wtile(sp_[0])
            for (sub, sbufs) in main:
                tail(sub, sbufs)
    flush()
    assert wstate["cons"] == len(wlist)
    S.emit(final_group=("sp", "o"))
    return nc, S


_CACHE = {}


def _pack_vecs(ffn1_norm, mix_norm, ffn2_norm, final_norm, conv_dw_b, conv_ln_g, conv_ln_b, pool_scale, conv_dw):
    v = np.zeros((128, NV), np.float32)

    def put(col, a):
        a = np.asarray(a, np.float32)
        k = a.shape[0] // 128
        v[:, col:col + k] = a.reshape(k, 128).T
    put(V_FFN1, ffn1_norm)
    put(V_MIX, mix_norm)
    put(V_FFN2, ffn2_norm)
    put(V_FIN, final_norm)
    put(V_CB, conv_dw_b)
    put(V_LNG, conv_ln_g)
    put(V_LNB, conv_ln_b)
    put(V_PSC, pool_scale)
    dw = np.asarray(conv_dw, np.float32)
    dwp = np.zeros((2 * NPAIR, D_CONV), np.float32)
    dwp[:CONV_W] = dw
    for c in range(CC):
        for hh in range(2):
            ch = dwp[:, c * 128 + 64 * hh:c * 128 + 64 * hh + 64]
            ev, od = ch[0::2].T, ch[1::2].T
            col = V_DW + (c * 2 + hh) * NPAIR
            if hh == 0:
                v[0:64, col:col + NPAIR], v[64:128, col:col + NPAIR] = ev, od
            else:
                v[0:64, col:col + NPAIR], v[64:128, col:col + NPAIR] = od, ev
    return v


def kernel(x, ffn1_norm, ffn1_w_gate, ffn1_w_up, ffn1_w_down, mix_norm, w_in,
           conv_dw, conv_dw_b, conv_ln_g, conv_ln_b, conv_pw, pool_w, pool_scale,
           w_out, ffn2_norm, ffn2_w_gate, ffn2_w_up, ffn2_w_down, final_norm):
    x = np.asarray(x, np.float32)
    Bn, T, Dm = x.shape
    n_cores = 8
    per_b = n_cores // Bn
    TOK = T // per_b
    ntiles = TOK // TT
    key = ntiles
    if key not in _CACHE:
        _CACHE[key] = build_program(ntiles)[0]
    nc = _CACHE[key]
    f32c = lambda a: np.ascontiguousarray(np.asarray(a, np.float32))
    shared = {
        "ident": np.ascontiguousarray(np.tile(np.eye(64, dtype=np.float32), (2, 1))),
        "vecs": _pack_vecs(ffn1_norm, mix_norm, ffn2_norm, final_norm, conv_dw_b, conv_ln_g, conv_ln_b,
                           pool_scale, conv_dw),
        "ffn1_w_gate": f32c(ffn1_w_gate), "ffn1_w_up": f32c(ffn1_w_up), "ffn1_w_down": f32c(ffn1_w_down),
        "ffn2_w_gate": f32c(ffn2_w_gate), "ffn2_w_up": f32c(ffn2_w_up), "ffn2_w_down": f32c(ffn2_w_down),
        "w_in": f32c(w_in), "conv_pw": f32c(conv_pw), "pool_w": f32c(pool_w).reshape(4 * 128, 128),
        "w_out": f32c(w_out),
    }
    in_maps = []
    for i in range(n_cores):
        b, part = divmod(i, per_b)
        t0 = part * TOK
        xt = np.zeros((Dm, HALO + TOK), np.float32)
        xt[:, HALO:] = x[b, t0:t0 + TOK, :].T
        if part > 0:
            xt[:, :HALO] = x[b, t0 - HALO:t0, :].T
        pos = np.broadcast_to((t0 + np.arange(16, dtype=np.float32))[None, :], (128, 16))
        m = dict(shared)
        m["xT"] = xt
        m["pos"] = np.ascontiguousarray(pos, dtype=np.float32)
        in_maps.append(m)
    res = run_bass_kernel_spmd(nc, in_maps, core_ids=list(range(n_cores)))
    out = np.empty((Bn, T, Dm), np.float32)
    for i in range(n_cores):
        b, part = divmod(i, per_b)
        t0 = part * TOK
        out[b, t0:t0 + TOK, :] = np.asarray(res.results[i]["outT"], np.float32).T
    return out
```

```python
import numpy as np
import concourse.bass as bass
import concourse.mybir as mybir
from concourse.bass_utils import run_bass_kernel_spmd

F32 = mybir.dt.float32
BF16 = mybir.dt.bfloat16
AF = mybir.ActivationFunctionType
ALU = mybir.AluOpType

D_MODEL = 1024
D_FF = 2816
D_CONV = 512
CONV_W = 31
KC = 8
FCH = 11
CC = 4
HALO = 32
SUBW = 512
TT = 1024
NSLOT = 6
RSQ = "lnexp"
RMS_EPS = 1e-6
LN_EPS = 1e-5

V_FFN1, V_MIX, V_FFN2, V_FIN = 0, 8, 16, 24
V_CB, V_LNG, V_LNB, V_PSC = 32, 36, 40, 44
V_DW = 48
NPAIR = 16
NV = V_DW + CC * 2 * NPAIR

COMPUTE = ("pe", "act", "dve", "pool")


class Buf:
    __slots__ = ("name", "lastw", "readers", "psum")

    def __init__(self, name, psum=False):
        self.name = name
        self.lastw = None
        self.readers = []
        self.psum = psum


class Sched:
    def __init__(self, nc, n_dma_sems=8):
        self.nc = nc
        self.prog = {e: [] for e in COMPUTE + ("sp",)}
        self.sem = {e: nc.alloc_semaphore(f"s_{e}") for e in COMPUTE}
        self.cnt = {e: 0 for e in COMPUTE}
        self.waited = {e: {} for e in COMPUTE + ("sp",)}
        self.dma_sems = {}
        self.n_dma_sems = n_dma_sems
        self.dma_rr = {}

    def _need(self, eng, tok, waits):
        if tok is None:
            return
        key, sem, val = tok
        if self.waited[eng].get(key, 0) >= val:
            return
        self.waited[eng][key] = val
        waits[key] = (sem, val)

    def _deps(self, eng, reads, writes):
        waits = {}
        for b in reads:
            self._need(eng, b.lastw, waits)
            if b.psum:
                for t in b.readers:
                    if t[0] != eng:
                        self._need(eng, t, waits)
        for b in writes:
            self._need(eng, b.lastw, waits)
            for t in b.readers:
                self._need(eng, t, waits)
        return list(waits.values())

    def _commit(self, tok, reads, writes):
        for b in reads:
            b.readers.append(tok)
            if len(b.readers) > 12:
                best = {}
                for t in b.readers:
                    if t[0] not in best or best[t[0]][2] < t[2]:
                        best[t[0]] = t
                b.readers = list(best.values())
        for b in writes:
            b.lastw = tok
            b.readers = []

    def op(self, eng, fn, reads=(), writes=()):
        waits = self._deps(eng, reads, writes)
        self.cnt[eng] += 1
        tok = (eng, self.sem[eng], self.cnt[eng])
        self.prog[eng].append((waits, fn, self.sem[eng]))
        self._commit(tok, reads, writes)
        return tok

    def dma(self, eng, fn, reads=(), writes=(), group="g"):
        k = (eng, group)
        if k not in self.dma_sems:
            self.dma_sems[k] = [[self.nc.alloc_semaphore(f"d_{eng}_{group}_{i}"), 0]
                                for i in range(self.n_dma_sems)]
            self.dma_rr[k] = 0
        i = self.dma_rr[k]
        self.dma_rr[k] = (i + 1) % len(self.dma_sems[k])
        ent = self.dma_sems[k][i]
        sem, prev = ent
        waits = self._deps(eng, reads, writes)
        if prev > 0:
            w2 = {}
            self._need(eng, ((eng, group, i), sem, prev), w2)
            waits += list(w2.values())
        ent[1] = prev + 16
        tok = ((eng, group, i), sem, prev + 16)
        self.prog[eng].append((waits, fn, ("dma", sem)))
        self._commit(tok, reads, writes)
        return tok

    def emit(self, final_group=None):
        names = {"pe": "tensor", "act": "scalar", "dve": "vector", "pool": "gpsimd", "sp": "sync"}
        final_waits = []
        if final_group is not None:
            final_waits = [(None, sem, val) for (sem, val) in self.dma_sems[final_group] if val > 0]
        with self.nc.Block() as block:
            for e, attr in names.items():
                prog = self.prog[e]
                fw = final_waits if e == "sp" else ()

                def section(engine, prog=prog, fw=fw):
                    for waits, fn, sem in prog:
                        for (s, v) in waits:
                            engine.wait_ge(s, v)
                        ins = fn(engine)
                        if isinstance(sem, tuple):
                            ins.then_inc(sem[1], 16)
                        else:
                            ins.then_inc(sem, 1)
                    for (_k, s, v) in fw:
                        engine.wait_ge(s, v)

                getattr(block, attr)(section)


class Sub:
    def __init__(self, name, c0, n, tok0, main):
        self.name, self.c0, self.n, self.tok0, self.main = name, c0, n, tok0, main
        self.sl = slice(c0, c0 + n)


def build_program(ntiles=4, bcast_diag=True, debug_stop=None):
    TOK = ntiles * TT
    nc = bass.Bass("TRN2", target_bir_lowering=False)
    S = Sched(nc)

    def din(name, shape):
        return nc.dram_tensor(name, list(shape), F32, kind="ExternalInput").ap()

    xT_d = din("xT", [D_MODEL, HALO + TOK])
    pos_d = din("pos", [128, 16])
    vecs_d = din("vecs", [128, NV])
    ident_d = din("ident", [128, 64])
    W = {}
    for f in ("ffn1", "ffn2"):
        W[f + "_g"] = din(f + "_w_gate", [D_MODEL, D_FF])
        W[f + "_u"] = din(f + "_w_up", [D_MODEL, D_FF])
        W[f + "_d"] = din(f + "_w_down", [D_FF, D_MODEL])
    W["w_in"] = din("w_in", [D_MODEL, 3 * D_CONV])
    W["conv_pw"] = din("conv_pw", [D_CONV, D_CONV])
    W["pool_w"] = din("pool_w", [4 * 128, 128])
    W["w_out"] = din("w_out", [D_MODEL, D_MODEL])
    outT_d = nc.dram_tensor("outT", [D_MODEL, TOK], F32, kind="ExternalOutput").ap()

    WCOLS = HALO + TT
    sb = nc.alloc_sbuf_tensor
    xT = sb("xT_sb", [128, KC, WCOLS], F32)
    hT = sb("hT_sb", [128, KC, WCOLS], BF16)
    actT = sb("actT_sb", [128, FCH, WCOLS], BF16)
    wring = [sb(f"wr{i}", [128, 8, 512], BF16) for i in range(NSLOT)]
    wring_b = [Buf(f"wr{i}") for i in range(NSLOT)]
    NBF = 4
    bfr = [sb(f"bfr{i}", [128, SUBW], BF16) for i in range(NBF)]
    bfr_b = [Buf(f"bfr{i}") for i in range(NBF)]
    NTMP = 3
    tmpf = [sb(f"tmpf{i}", [128, SUBW], F32) for i in range(NTMP)]
    tmpf_b = [Buf(f"tmpf{i}") for i in range(NTMP)]
    NST = 2
    stt = [sb(f"stt{i}", [128, SUBW], F32) for i in range(NST)]
    stt_b = [Buf(f"stt{i}") for i in range(NST)]
    lnM = sb("lnM", [128, SUBW], F32)
    lnV = sb("lnV", [128, SUBW], F32)
    lnM_b, lnV_b = Buf("lnM"), Buf("lnV")
    ubuf = [sb(f"ubuf{i}", [128, CC, 2, HALO + SUBW], BF16) for i in range(2)]
    ubuf_b = [[Buf(f"u{i}_{c}") for c in range(CC)] for i in range(2)]
    urep_b = [[Buf(f"ur{i}_{c}") for c in range(CC)] for i in range(2)]
    uhist_b = [Buf(f"uh{i}") for i in range(2)]
    pbuf = [sb(f"pbuf{i}", [128, CC, 16 + SUBW], F32) for i in range(2)]
    pbuf_b = [[Buf(f"p{i}_{c}") for c in range(CC)] for i in range(2)]
    phist_b = [Buf(f"ph{i}") for i in range(2)]
    psc = [sb(f"psc{i}", [128, 16 + SUBW], F32) for i in range(2)]
    psc_b = [Buf(f"psc{i}") for i in range(2)]
    convo = sb("convo", [128, CC, SUBW], F32)
    convo_b = [Buf(f"convo{c}") for c in range(CC)]
    yb = sb("yb", [128, CC, SUBW], BF16)
    yb_b = [Buf(f"yb{c}") for c in range(CC)]
    NMX = 4
    mixed = [sb(f"mixed{i}", [128, SUBW], BF16) for i in range(NMX)]
    mixed_b = [Buf(f"mixed{i}") for i in range(NMX)]
    catT = sb("catT", [128, 2 * CC, SUBW], BF16)
    catT_b = [Buf(f"cat{c}") for c in range(2 * CC)]
    NDG = 3
    NDGD, NDGP = 2, 4
    dgr = [sb(f"dgr{i}", [128, 8, 64], BF16) for i in range(NDGD)]
    dgr_b = [Buf(f"dgr{i}") for i in range(NDGD)]
    dgp = [sb(f"dgp{i}", [128, 8, 64], BF16) for i in range(NDGP)]
    dgp_b = [Buf(f"dgp{i}") for i in range(NDGP)]
    vecs = sb("vecs_sb", [128, NV], F32)
    vecs_b = Buf("vecs")
    wh = sb("wh_sb", [128, CC * 2 * NPAIR], F32)
    wh_b = Buf("wh")
    ones = sb("ones_sb", [128, 128], BF16)
    ones_b = Buf("ones")
    identf = sb("identf_sb", [128, 64], F32)
    ident_b = Buf("ident")
    epsc = sb("epsc_sb", [128, 2], F32)
    negh_b = Buf("epsc")
    rcnt = sb("rcnt_sb", [128, CC, 16], F32)
    rcnt_b = Buf("rcnt")

    NBANK = 8
    banks = [nc.alloc_psum_tensor(f"bank{i}", [128, SUBW], F32) for i in range(NBANK)]
    banks_b = [Buf(f"bank{i}", psum=True) for i in range(NBANK)]
    rr = {"bank": 0, "bfr": 0, "tmp": 0, "stt": 0, "mixed": 0, "dgr": 0, "dgp": 0, "psc": 0}

    def nxt(kind, n):
        i = rr[kind]
        rr[kind] = (i + 1) % n
        return i

    held = set()

    def bank(hold=False):
        for _ in range(NBANK):
            i = nxt("bank", NBANK)
            if i not in held:
                break
        else:
            raise RuntimeError("all PSUM banks held")
        if hold:
            held.add(i)
        return banks[i], banks_b[i]

    def release(bk_b):
        held.discard(banks_b.index(bk_b))

    S.dma("sp", lambda e: e.dma_start(out=vecs[:], in_=vecs_d), writes=[vecs_b], group="c")
    posb, pos_b = lnV[:, 0:16], lnV_b
    fix, fix_b = lnM[:, 0:16], lnM_b
    S.dma("sp", lambda e: e.dma_start(out=posb, in_=pos_d), writes=[pos_b], group="c")
    S.op("dve", lambda v: v.memset(ones[:], 1.0), writes=[ones_b])
    S.op("dve", lambda v: v.memset(epsc[:, 0:1], RMS_EPS), writes=[negh_b])
    S.op("dve", lambda v: v.memset(epsc[:, 1:2], LN_EPS), reads=[negh_b], writes=[negh_b])
    S.dma("sp", lambda e: e.dma_start(out=identf[:], in_=ident_d), writes=[ident_b], group="c")
    for i in range(2):
        S.op("dve", lambda v, i=i: v.memset(psc[i][:], 0.0), writes=[psc_b[i]])
        S.op("pool", lambda g, i=i: g.memset(ubuf[i][:], 0.0),
             writes=ubuf_b[i] + urep_b[i] + [uhist_b[i]])
    S.op("dve", lambda v: v.tensor_scalar(out=wh[:], in0=vecs[:, V_DW:V_DW + CC * 2 * NPAIR], scalar1=0.5,
                                          scalar2=None, op0=ALU.mult), reads=[vecs_b], writes=[wh_b])
    for g in range(CC):
        S.op("dve", lambda v, g=g: v.tensor_scalar(out=rcnt[:, g, :], in0=posb, scalar1=1.0,
                                                   scalar2=float(2 ** (g + 1)), op0=ALU.add, op1=ALU.min),
             reads=[pos_b], writes=[rcnt_b])
    S.op("dve", lambda v: v.reciprocal(out=rcnt[:], in_=rcnt[:]), reads=[rcnt_b], writes=[rcnt_b])

    def wspecs_ffn(f):
        specs = []
        for half in range(2):
            f0 = half * FCH
            groups = [(g0, min(4, FCH - g0)) for g0 in range(0, FCH, 4)]
            for q, (g0, nf) in enumerate(groups):
                if half == 0 and q < 2:
                    rels = (4 - 2 * q, 3 - 2 * q)
                else:
                    rels = (2, 1)
                for nm, rel in zip(("_g", "_u"), rels):
                    specs.append(((f + nm, 0, 8, (f0 + g0) * 128, nf * 128), rel))
            for dg in range(2):
                for k0, r in ((0, 2), (8, 1)):
                    nk = min(8, FCH - k0)
                    rel = r if half == 0 else (4 - 2 * dg - (0 if k0 == 0 else 1))
                    specs.append(((f + "_d", (f0 + k0) * 128, nk, dg * 512, 512), rel))
        return specs

    def wspecs_mixer():
        return [(("w_in", 0, 8, 0, 512), 3), (("w_in", 0, 8, 512, 512), 2), (("w_in", 0, 8, 1024, 512), 1),
                (("conv_pw", 0, 4, 0, 512), 4), (("pool_w", 0, 4, 0, 128), 3),
                (("w_out", 0, 8, 0, 512), 2), (("w_out", 0, 8, 512, 512), 1)]

    wlist = []
    for _j in range(ntiles):
        wlist += wspecs_ffn("ffn1") + wspecs_mixer() + wspecs_ffn("ffn2")
    wstate = {"loaded": 0, "cons": 0}
    startup_gate = []

    def w_issue(i):
        name, r0, nk, c0, ncols = wlist[i][0]
        slot = i % NSLOT
        src = W[name][r0:r0 + nk * 128, c0:c0 + ncols].rearrange("(k p) c -> p k c", p=128)
        dst = wring[slot][:, 0:nk, 0:ncols]
        gate = list(startup_gate) if 4 <= i < NSLOT else []
        S.dma("pool", lambda e, dst=dst, src=src: e.dma_start(out=dst, in_=src),
              reads=gate, writes=[wring_b[slot]], group="w")

    def wtile(spec):
        i = wstate["cons"]
        assert wlist[i][0] == spec, (i, wlist[i], spec)
        while wstate["loaded"] < min(len(wlist), i + NSLOT):
            m = wstate["loaded"]
            if m >= NSLOT:
                k = m - NSLOT
                if k + wlist[k][1] > i:
                    break
            w_issue(m)
            wstate["loaded"] += 1
        assert wstate["loaded"] > i
        wstate["cons"] += 1
        return wring[i % NSLOT], wring_b[i % NSLOT]

    class SB:
        pass

    def mk_sub_bufs(sub):
        o = SB()
        o.x = [Buf(f"x{c}{sub.name}") for c in range(KC)]
        o.h = [Buf(f"h{c}{sub.name}") for c in range(KC)]
        o.a = [Buf(f"a{c}{sub.name}") for c in range(FCH)]
        return o

    def mm_group(ps_ap, pairs, reads, bank_b):
        def fn(t, pairs=pairs, ps_ap=ps_ap):
            ins = None
            n = len(pairs)
            for i, (l, r) in enumerate(pairs):
                ins = t.matmul(ps_ap, lhsT=l, rhs=r, start=(i == 0), stop=(i == n - 1))
            return ins
        S.op("pe", fn, reads=reads, writes=[bank_b])

    def norm_sums(sub, sbufs):
        n = sub.n
        bk, bk_b = bank(hold=True)
        for c in range(KC):
            i = nxt("bfr", NBF)
            S.op("act", lambda a, i=i, c=c: a.activation(out=bfr[i][:, 0:n], in_=xT[:, c, sub.sl], func=AF.Square),
                 reads=[sbufs.x[c]], writes=[bfr_b[i]])
            S.op("pe", lambda t, i=i, c=c: t.matmul(bk[:, 0:n], lhsT=ones[:], rhs=bfr[i][:, 0:n],
                                                    start=(c == 0), stop=(c == KC - 1)),
                 reads=[bfr_b[i], ones_b], writes=[bk_b])
        return bk, bk_b

    def norm_rstd(sub, bk, bk_b):
        n = sub.n
        si = nxt("stt", NST)
        S.op("act", lambda a: a.activation(out=stt[si][:, 0:n], in_=bk[:, 0:n], func=AF.Ln,
                                           scale=1.0 / D_MODEL, bias=epsc[:, 0:1]),
             reads=[bk_b, negh_b], writes=[stt_b[si]])
        release(bk_b)
        S.op("act", lambda a: a.activation(out=stt[si][:, 0:n], in_=stt[si][:, 0:n], func=AF.Exp, scale=-0.5),
             reads=[stt_b[si]], writes=[stt_b[si]])
        return stt[si], stt_b[si]

    def norm_stats(sub, sbufs):
        bk, bk_b = norm_sums(sub, sbufs)
        return norm_rstd(sub, bk, bk_b)

    def h_ops(sub, sbufs, vcol, r, r_b):
        n = sub.n
        for c in range(KC):
            S.op("dve", lambda v, c=c: v.scalar_tensor_tensor(out=hT[:, c, sub.sl], in0=xT[:, c, sub.sl],
                                                              scalar=vecs[:, vcol + c:vcol + c + 1], in1=r[:, 0:n],
                                                              op0=ALU.mult, op1=ALU.mult),
                 reads=[sbufs.x[c], r_b, vecs_b], writes=[sbufs.h[c]])

    def norm_to_h_2(sub, sbufs, vcol):
        bk, bk_b = norm_sums(sub, sbufs)

        def part2():
            r, r_b = norm_rstd(sub, bk, bk_b)
            h_ops(sub, sbufs, vcol, r, r_b)
        return part2

    def norm_to_h(sub, sbufs, vcol):
        r, r_b = norm_stats(sub, sbufs)
        h_ops(sub, sbufs, vcol, r, r_b)

    pending = []

    def flush():
        while pending:
            r_ = pending.pop(0)()
            if callable(r_):
                r_()

    def ffn(f, subs, hook=None):
        def up_chunk(sub, sbufs, wg, wg_b, wu, wu_b, fi, fl):
            n = sub.n
            bg, bg_b = bank()
            mm_group(bg[:, 0:n], [(wg[:, k, fi * 128:(fi + 1) * 128], hT[:, k, sub.sl]) for k in range(KC)],
                     [wg_b] + sbufs.h, bg_b)
            bu, bu_b = bank()
            mm_group(bu[:, 0:n], [(wu[:, k, fi * 128:(fi + 1) * 128], hT[:, k, sub.sl]) for k in range(KC)],
                     [wu_b] + sbufs.h, bu_b)
            ti = nxt("tmp", NTMP)
            S.op("act", lambda a: a.activation(out=tmpf[ti][:, 0:n], in_=bg[:, 0:n], func=AF.Silu),
                 reads=[bg_b], writes=[tmpf_b[ti]])
            S.op("dve", lambda v: v.tensor_tensor(out=actT[:, fl, sub.sl], in0=bu[:, 0:n], in1=tmpf[ti][:, 0:n],
                                                  op=ALU.mult),
                 reads=[bu_b, tmpf_b[ti]], writes=[sbufs.a[fl]])

        def down_chunk(sub, sbufs, wd0, wd0_b, wd1, wd1_b, dg, dl):
            n = sub.n
            dc = dg * 4 + dl
            bk, bk_b = bank()
            pairs = []
            for fl in range(FCH):
                wt = wd0 if fl < 8 else wd1
                pairs.append((wt[:, fl % 8, dl * 128:(dl + 1) * 128], actT[:, fl, sub.sl]))
            mm_group(bk[:, 0:n], pairs, [wd0_b, wd1_b] + sbufs.a, bk_b)
            S.op("dve", lambda v: v.scalar_tensor_tensor(out=xT[:, dc, sub.sl], in0=bk[:, 0:n], scalar=0.5,
                                                         in1=xT[:, dc, sub.sl], op0=ALU.mult, op1=ALU.add),
                 reads=[bk_b, sbufs.x[dc]], writes=[sbufs.x[dc]])

        for half in range(2):
            f0 = half * FCH
            groups = []
            for g0 in range(0, FCH, 4):
                nf = min(4, FCH - g0)
                groups.append((g0, nf))
            gi = 0
            if half == 0:
                tl = []
                for (g0, nf) in groups[:2]:
                    tl.append(wtile((f + "_g", 0, 8, (f0 + g0) * 128, nf * 128)))
                    tl.append(wtile((f + "_u", 0, 8, (f0 + g0) * 128, nf * 128)))
                for (sub, sbufs) in subs:
                    for q, (g0, nf) in enumerate(groups[:2]):
                        (wg, wg_b), (wu, wu_b) = tl[2 * q], tl[2 * q + 1]
                        for fi in range(nf):
                            up_chunk(sub, sbufs, wg, wg_b, wu, wu_b, fi, g0 + fi)
                            if sub.main and q == 1 and fi == 1:
                                flush()
                gi = 2
            for (g0, nf) in groups[gi:]:
                wg, wg_b = wtile((f + "_g", 0, 8, (f0 + g0) * 128, nf * 128))
                wu, wu_b = wtile((f + "_u", 0, 8, (f0 + g0) * 128, nf * 128))
                for (sub, sbufs) in subs:
                    for fi in range(nf):
                        up_chunk(sub, sbufs, wg, wg_b, wu, wu_b, fi, g0 + fi)
            if half == 0:
                for dg in range(2):
                    wd0, wd0_b = wtile((f + "_d", (f0 + 0) * 128, 8, dg * 512, 512))
                    wd1, wd1_b = wtile((f + "_d", (f0 + 8) * 128, FCH - 8, dg * 512, 512))
                    for (sub, sbufs) in subs:
                        for dl in range(4):
                            down_chunk(sub, sbufs, wd0, wd0_b, wd1, wd1_b, dg, dl)
            else:
                tl = []
                for dg in range(2):
                    tl.append(wtile((f + "_d", (f0 + 0) * 128, 8, dg * 512, 512)))
                    tl.append(wtile((f + "_d", (f0 + 8) * 128, FCH - 8, dg * 512, 512)))
                for (sub, sbufs) in subs:
                    for dg in range(2):
                        (wd0, wd0_b), (wd1, wd1_b) = tl[2 * dg], tl[2 * dg + 1]
                        for dl in range(4):
                            down_chunk(sub, sbufs, wd0, wd0_b, wd1, wd1_b, dg, dl)
                            if dg == 1 and dl == 1 and sub.main:
                                flush()
                    if hook is not None:
                        p2 = hook(sub, sbufs)
                        if p2 is not None:
                            pending.append(p2)

    def mixer(subs, first_of_core, hook):
        wa, wa_b = wtile(("w_in", 0, 8, 0, 512))
        wg, wg_b = wtile(("w_in", 0, 8, 512, 512))
        wp, wp_b = wtile(("w_in", 0, 8, 1024, 512))
        first_main = [True]
        deferred = []

        def hist_copy(ui):
            nu = 1 - ui
            S.op("pool", lambda g: g.tensor_copy(out=ubuf[nu][0:64, :, 0, 0:HALO],
                                                 in_=ubuf[ui][0:64, :, 0, SUBW:SUBW + HALO]),
                 reads=ubuf_b[ui], writes=[uhist_b[nu]])
            S.op("pool", lambda g: g.tensor_copy(out=ubuf[nu][64:128, :, 1, 0:HALO],
                                                 in_=ubuf[ui][64:128, :, 1, SUBW:SUBW + HALO]),
                 reads=ubuf_b[ui], writes=[uhist_b[nu]])
            S.op("pool", lambda g: g.tensor_copy(out=pbuf[nu][:, :, 0:16], in_=pbuf[ui][:, :, SUBW:SUBW + 16]),
                 reads=pbuf_b[ui], writes=[phist_b[nu]])

        msubs = [s_ for s_ in subs if s_[0].main]
        sts = [MixState(sub, sbufs, ui) for (sub, sbufs, ui) in msubs]
        A_, B_ = sts
        for (sub, sbufs, ui) in subs:
            n = sub.n
            ucol = HALO if sub.main else 0
            for c in range(CC):
                ba, ba_b = bank()
                mm_group(ba[:, 0:n], [(wa[:, k, c * 128:(c + 1) * 128], hT[:, k, sub.sl]) for k in range(KC)],
                         [wa_b] + sbufs.h, ba_b)
                bg, bg_b = bank()
                mm_group(bg[:, 0:n], [(wg[:, k, c * 128:(c + 1) * 128], hT[:, k, sub.sl]) for k in range(KC)],
                         [wg_b] + sbufs.h, bg_b)
                ti = nxt("tmp", NTMP)
                S.op("act", lambda a, ti=ti, bg=bg, n=n: a.activation(out=tmpf[ti][:, 0:n], in_=bg[:, 0:n],
                                                                      func=AF.Tanh, scale=0.5),
                     reads=[bg_b], writes=[tmpf_b[ti]])
                uw = [ubuf_b[ui][c]] if sub.main else [uhist_b[ui]]
                for hh in range(2):
                    ps_ = slice(64 * hh, 64 * hh + 64)
                    S.op("dve", lambda v, ti=ti, ba=ba, n=n, c=c, ui=ui, ucol=ucol, hh=hh, ps_=ps_: v.scalar_tensor_tensor(
                        out=ubuf[ui][ps_, c, hh, ucol:ucol + n], in0=tmpf[ti][ps_, 0:n], scalar=1.0, in1=ba[ps_, 0:n],
                        op0=ALU.add, op1=ALU.mult),
                         reads=[ba_b, tmpf_b[ti]], writes=uw)
                if sub.main:
                    WU = HALO + SUBW
                    S.dma("sp", lambda e, c=c, ui=ui: e.dma_start(out=ubuf[ui][64:128, c, 0, 0:WU - 1],
                                                                  in_=ubuf[ui][0:64, c, 0, 1:WU]),
                          reads=[ubuf_b[ui][c], uhist_b[ui]], writes=[urep_b[ui][c]], group="r")
                    S.dma("sp", lambda e, c=c, ui=ui: e.dma_start(out=ubuf[ui][0:64, c, 1, 0:WU - 1],
                                                                  in_=ubuf[ui][64:128, c, 1, 1:WU]),
                          reads=[ubuf_b[ui][c], uhist_b[ui], urep_b[ui][c]], writes=[urep_b[ui][c]], group="r")
                if sub.main and first_main[0] and c == 1:
                    flush()
            for c in range(CC):
                bp, bp_b = bank()
                mm_group(bp[:, 0:n], [(wp[:, k, c * 128:(c + 1) * 128], hT[:, k, sub.sl]) for k in range(KC)],
                         [wp_b] + sbufs.h, bp_b)
                if sub.main:
                    S.op("act", lambda a, bp=bp, c=c, ui=ui, n=n: a.activation(
                        out=pbuf[ui][:, c, 16:16 + n], in_=bp[:, 0:n], func=AF.Identity),
                         reads=[bp_b], writes=[pbuf_b[ui][c]])
                else:
                    S.op("act", lambda a, bp=bp, c=c, ui=ui: a.activation(
                        out=pbuf[ui][:, c, 0:16], in_=bp[:, 16:32], func=AF.Identity),
                         reads=[bp_b], writes=[phist_b[ui]])
            if sub.main:
                if first_main[0]:
                    hist_copy(ui)
                    first_main[0] = False
                    pool_dve(A_, first_of_core, eng="pool")
                else:
                    deferred.append(ui)
        wpw, wpw_b = wtile(("conv_pw", 0, 4, 0, 512))
        wpl, wpl_b = wtile(("pool_w", 0, 4, 0, 128))
        wos = [wtile(("w_out", 0, 8, 0, 512)), wtile(("w_out", 0, 8, 512, 512))]
        for c in range(CC):
            conv_chunk(A_, c)
        B_.all_pool = True
        conv_chunk(B_, 0)
        evac_part(A_)
        ln_apply(A_)
        pool_pe(A_, wpl, wpl_b)
        conv_chunk(B_, 1)
        conv_chunk(B_, 2)
        conv_chunk(B_, 3)
        pw_stage(A_, wpw, wpw_b, evac=False)
        evac_part(B_)
        pw_evac(A_)
        ln_apply(B_)
        pool_dve(B_, False, groups=(3,))
        wout_stage(A_, wos)
        pool_dve(B_, False, groups=(2, 1, 0))
        p2 = hook(A_.sub, A_.sbufs)
        while deferred:
            hist_copy(deferred.pop(0))
        pw_stage(B_, wpw, wpw_b)
        p2b = p2() if p2 is not None else None
        pool_pe(B_, wpl, wpl_b)
        if callable(p2b):
            p2b()
        wout_stage(B_, wos)
        p2 = hook(B_.sub, B_.sbufs)
        if p2 is not None:
            pending.append(p2)

    class MixState:
        def __init__(self, sub, sbufs, ui):
            self.sub, self.sbufs, self.ui, self.n = sub, sbufs, ui, sub.n
            self.cbanks = [None] * CC
            self.done_c = 0
            self.all_pool = False

    def conv_chunk(st, c):
        n, ui = st.n, st.ui
        bk, bk_b = bank(hold=True)
        st.cbanks[c] = (bk, bk_b)
        for r in range(2):
            tl = []
            for hh in range(2):
                on_pool = (st.all_pool and c >= 1) or (hh == 1)
                if on_pool:
                    di = nxt("dgp", NDGP)
                    dt_, dt_b = dgp[di], dgp_b[di]
                else:
                    di = nxt("dgr", NDGD)
                    dt_, dt_b = dgr[di], dgr_b[di]
                base = (c * 2 + hh) * NPAIR + r * 8
                wsl = wh[:, base:base + 8]
                S.op("pool" if on_pool else "dve", lambda v, dt_=dt_, wsl=wsl: v.tensor_tensor(
                    out=dt_[:], in0=identf[:].unsqueeze(1).to_broadcast([128, 8, 64]),
                    in1=wsl.unsqueeze(2).to_broadcast([128, 8, 64]), op=ALU.mult),
                     reads=[ident_b, wh_b], writes=[dt_b])
                tl.append((dt_, dt_b))

            def fn(t, tl=tl, r=r):
                ins = None
                for jl in range(8):
                    jj = r * 8 + jl
                    for hh in range(2):
                        ins = t.matmul(bk[64 * hh:64 * hh + 64, 0:n], lhsT=tl[hh][0][:, jl, :],
                                       rhs=ubuf[ui][:, c, hh, 2 + 2 * jj:2 + 2 * jj + n],
                                       start=(jj == 0), stop=(jj == NPAIR - 1), tile_position=(0, 64 * hh))
                return ins
            S.op("pe", fn, reads=[tl[0][1], tl[1][1], ubuf_b[ui][c], urep_b[ui][c], uhist_b[ui]], writes=[bk_b])

    def evac_part(st):
        n = st.n
        st.s1 = None
        st.bfi = []
        for c in range(CC):
            bk, bk_b = st.cbanks[c]
            bcol = vecs[:, V_CB + c:V_CB + c + 1]
            i1 = nxt("bfr", NBF)
            S.op("act", lambda a, bk=bk, i1=i1, bcol=bcol: a.activation(out=bfr[i1][:, 0:n], in_=bk[:, 0:n],
                                                                        func=AF.Identity, bias=bcol),
                 reads=[bk_b, vecs_b], writes=[bfr_b[i1]])
            i2 = nxt("bfr", NBF)
            S.op("act", lambda a, bk=bk, i2=i2, bcol=bcol: a.activation(out=bfr[i2][:, 0:n], in_=bk[:, 0:n],
                                                                        func=AF.Square, bias=bcol),
                 reads=[bk_b, vecs_b], writes=[bfr_b[i2]])
            S.op("dve", lambda v, bk=bk, c=c, bcol=bcol: v.tensor_scalar(out=convo[:, c, 0:n], in0=bk[:, 0:n],
                                                                         scalar1=bcol, scalar2=None, op0=ALU.add),
                 reads=[bk_b, vecs_b, bfr_b[i1], bfr_b[i2]], writes=[convo_b[c]])
            release(bk_b)
            st.bfi.append((i1, i2))
            if len(st.bfi) == 2 or c == CC - 1:
                if st.s1 is None:
                    st.s1, st.s1_b = bank(hold=True)
                    st.s2, st.s2_b = bank(hold=True)
                stats_pe(st)

    def stats_pe(st):
        n = st.n
        s1, s1_b, s2, s2_b = st.s1, st.s1_b, st.s2, st.s2_b
        while st.bfi:
            c = st.done_c
            i1, i2 = st.bfi.pop(0)
            S.op("pe", lambda t, i1=i1, c=c: t.matmul(s1[:, 0:n], lhsT=ones[:], rhs=bfr[i1][:, 0:n],
                                                      start=(c == 0), stop=(c == CC - 1)),
                 reads=[bfr_b[i1], ones_b], writes=[s1_b])
            S.op("pe", lambda t, i2=i2, c=c: t.matmul(s2[:, 0:n], lhsT=ones[:], rhs=bfr[i2][:, 0:n],
                                                      start=(c == 0), stop=(c == CC - 1)),
                 reads=[bfr_b[i2], ones_b], writes=[s2_b])
            st.done_c += 1

    def ln_apply(st):
        n = st.n
        s1, s1_b, s2, s2_b = st.s1, st.s1_b, st.s2, st.s2_b
        S.op("dve", lambda v: v.tensor_scalar(out=lnM[:, 0:n], in0=s1[:, 0:n], scalar1=1.0 / D_CONV, scalar2=None,
                                              op0=ALU.mult), reads=[s1_b], writes=[lnM_b])
        S.op("dve", lambda v: v.tensor_tensor(out=lnV[:, 0:n], in0=lnM[:, 0:n], in1=lnM[:, 0:n], op=ALU.mult),
             reads=[lnM_b], writes=[lnV_b])
        S.op("dve", lambda v: v.scalar_tensor_tensor(out=lnV[:, 0:n], in0=s2[:, 0:n], scalar=1.0 / D_CONV,
                                                     in1=lnV[:, 0:n], op0=ALU.mult, op1=ALU.subtract),
             reads=[s2_b, lnV_b], writes=[lnV_b])
        release(s1_b)
        release(s2_b)
        if RSQ == "lnexp":
            S.op("act", lambda a: a.activation(out=lnV[:, 0:n], in_=lnV[:, 0:n], func=AF.Ln, bias=epsc[:, 1:2]),
                 reads=[lnV_b, negh_b], writes=[lnV_b])
            S.op("act", lambda a: a.activation(out=lnV[:, 0:n], in_=lnV[:, 0:n], func=AF.Exp, scale=-0.5),
                 reads=[lnV_b], writes=[lnV_b])
        else:
            S.op("act", lambda a: a.activation(out=lnV[:, 0:n], in_=lnV[:, 0:n], func=AF.Sqrt, bias=epsc[:, 1:2]),
                 reads=[lnV_b, negh_b], writes=[lnV_b])
            S.op("dve", lambda v: v.reciprocal(out=lnV[:, 0:n], in_=lnV[:, 0:n]), reads=[lnV_b], writes=[lnV_b])
        S.op("dve", lambda v: v.scalar_tensor_tensor(out=lnM[:, 0:n], in0=lnM[:, 0:n], scalar=-1.0,
                                                     in1=lnV[:, 0:n], op0=ALU.mult, op1=ALU.mult),
             reads=[lnM_b, lnV_b], writes=[lnM_b])
        for c in range(CC):
            ti = nxt("tmp", NTMP)
            S.op("dve", lambda v, c=c, ti=ti: v.tensor_tensor(out=tmpf[ti][:, 0:n], in0=convo[:, c, 0:n],
                                                              in1=lnV[:, 0:n], op=ALU.mult),
                 reads=[convo_b[c], lnV_b], writes=[tmpf_b[ti]])
            S.op("dve", lambda v, ti=ti: v.tensor_tensor(out=tmpf[ti][:, 0:n], in0=tmpf[ti][:, 0:n],
                                                         in1=lnM[:, 0:n], op=ALU.add),
                 reads=[tmpf_b[ti], lnM_b], writes=[tmpf_b[ti]])
            S.op("act", lambda a, c=c, ti=ti: a.activation(out=yb[:, c, 0:n], in_=tmpf[ti][:, 0:n], func=AF.Silu,
                                                           scale=vecs[:, V_LNG + c:V_LNG + c + 1],
                                                           bias=vecs[:, V_LNB + c:V_LNB + c + 1]),
                 reads=[tmpf_b[ti], vecs_b], writes=[yb_b[c]])

    def pw_stage(st, wpw, wpw_b, evac=True):
        n = st.n
        st.pwb = []
        for co in range(CC):
            bk, bk_b = bank(hold=True)
            mm_group(bk[:, 0:n], [(wpw[:, k, co * 128:(co + 1) * 128], yb[:, k, 0:n]) for k in range(CC)],
                     [wpw_b] + yb_b, bk_b)
            st.pwb.append((bk, bk_b))
        if evac:
            pw_evac(st)

    def pw_evac(st):
        n = st.n
        for co, (bk, bk_b) in enumerate(st.pwb):
            S.op("act", lambda a, bk=bk, co=co: a.activation(out=catT[:, co, 0:n], in_=bk[:, 0:n], func=AF.Copy),
                 reads=[bk_b], writes=[catT_b[co]])
            release(bk_b)

    def pool_dve(st, fixup, groups=(3, 2, 1, 0), eng="dve"):
        n, ui = st.n, st.ui
        L = 16 + n
        if not hasattr(st, "mix"):
            st.mix = []
        for g in groups:
            src, src_b = pbuf[ui][:, g, :], [pbuf_b[ui][g], phist_b[ui]]
            sh = 1
            for _lvl in range(g + 1):
                pi = nxt("psc", 2)
                S.op(eng, lambda v, src=src, sh=sh, pi=pi: v.tensor_tensor(
                    out=psc[pi][:, sh:L], in0=src[:, sh:L], in1=src[:, 0:L - sh], op=ALU.add),
                     reads=src_b, writes=[psc_b[pi]])
                src, src_b = psc[pi], [psc_b[pi]]
                sh *= 2
            mi = nxt("mixed", NMX)
            wdt = float(2 ** (g + 1))
            S.op("dve", lambda v, src=src, mi=mi, g=g, wdt=wdt: v.scalar_tensor_tensor(
                out=mixed[mi][:, 0:n], in0=src[:, 16:16 + n], scalar=1.0 / wdt, in1=pbuf[ui][:, g, 16:16 + n],
                op0=ALU.mult, op1=ALU.subtract),
                 reads=src_b + [pbuf_b[ui][g]], writes=[mixed_b[mi]])
            if fixup:
                S.op("dve", lambda v, src=src, g=g: v.tensor_tensor(out=fix, in0=src[:, 16:32], in1=rcnt[:, g, :],
                                                                    op=ALU.mult),
                     reads=src_b + [rcnt_b], writes=[fix_b])
                S.op("dve", lambda v, mi=mi, g=g: v.tensor_tensor(out=mixed[mi][:, 0:16], in0=fix,
                                                                  in1=pbuf[ui][:, g, 16:32], op=ALU.subtract),
                     reads=[fix_b, pbuf_b[ui][g]], writes=[mixed_b[mi]])
            st.mix.append((g, mi))

    def pool_pe(st, wpl, wpl_b):
        n = st.n
        for (g, mi) in st.mix:
            bk, bk_b = bank()
            mm_group(bk[:, 0:n], [(wpl[:, g, 0:128], mixed[mi][:, 0:n])], [wpl_b, mixed_b[mi]], bk_b)
            S.op("act", lambda a, bk=bk, g=g: a.activation(out=catT[:, CC + g, 0:n], in_=bk[:, 0:n], func=AF.Identity,
                                                           scale=vecs[:, V_PSC + g:V_PSC + g + 1]),
                 reads=[bk_b, vecs_b], writes=[catT_b[CC + g]])

    def wout_stage(st, wos):
        n, sub, sbufs = st.n, st.sub, st.sbufs
        for dc in range(KC):
            wo, wo_b = wos[dc // 4]
            dl = dc % 4
            bk, bk_b = bank()
            mm_group(bk[:, 0:n], [(wo[:, k, dl * 128:(dl + 1) * 128], catT[:, k, 0:n]) for k in range(2 * CC)],
                     [wo_b] + catT_b, bk_b)
            S.op("dve", lambda v, bk=bk, dc=dc: v.tensor_tensor(
                out=xT[:, dc, sub.sl], in0=bk[:, 0:n], in1=xT[:, dc, sub.sl], op=ALU.add),
                 reads=[bk_b, sbufs.x[dc]], writes=[sbufs.x[dc]])

    out_tokens = []

    def final_out(sub, sbufs):
        n = sub.n
        if debug_stop is not None:
            for c in range(KC):
                tok = S.dma("sp", lambda e, c=c: e.dma_start(out=outT_d[c * 128:(c + 1) * 128, sub.tok0:sub.tok0 + n],
                                                             in_=xT[:, c, sub.sl]),
                            reads=[sbufs.x[c]], group="o")
            return
        r, r_b = norm_stats(sub, sbufs)
        for c in range(KC):
            S.op("dve", lambda v, c=c: v.scalar_tensor_tensor(out=xT[:, c, sub.sl], in0=xT[:, c, sub.sl],
                                                              scalar=vecs[:, V_FIN + c:V_FIN + c + 1], in1=r[:, 0:n],
                                                              op0=ALU.mult, op1=ALU.mult),
                 reads=[sbufs.x[c], r_b, vecs_b], writes=[sbufs.x[c]])
            S.dma("sp", lambda e, c=c: e.dma_start(out=outT_d[c * 128:(c + 1) * 128, sub.tok0:sub.tok0 + n],
                                                   in_=xT[:, c, sub.sl]),
                  reads=[sbufs.x[c]], group="o")

    gsub = 0
    slotA = Sub("A", HALO, SUBW, 0, True)
    slotB = Sub("B", HALO + SUBW, SUBW, 0, True)
    slotH = Sub("H", 0, HALO, 0, False)
    bufsA, bufsB, bufsH = mk_sub_bufs(slotA), mk_sub_bufs(slotB), mk_sub_bufs(slotH)
    def load_x(sub, sbufs):
        for c in range(KC):
            d0 = (HALO + sub.tok0) if sub.main else 0
            S.dma("sp", lambda e, c=c, d0=d0: e.dma_start(
                out=xT[:, c, sub.sl], in_=xT_d[c * 128:(c + 1) * 128, d0:d0 + sub.n]),
                  writes=[sbufs.x[c]], group="x")

    for j in range(ntiles):
        A = Sub("A", HALO, SUBW, j * TT, True)
        B = Sub("B", HALO + SUBW, SUBW, j * TT + SUBW, True)
        subs = [(A, bufsA), (B, bufsB)]
        if j == 0:
            subs = [(slotH, bufsH)] + subs
            for (sub, sbufs) in subs:
                load_x(sub, sbufs)
            startup_gate.extend(bufsA.x)
            for (sub, sbufs) in subs:
                norm_to_h(sub, sbufs, V_FFN1)
        ffn("ffn1", subs, hook=lambda sub, sbufs: (lambda: norm_to_h(sub, sbufs, V_MIX)))
        msubs = []
        for (sub, sbufs) in subs:
            if sub.main:
                msubs.append((sub, sbufs, gsub % 2))
                gsub += 1
            else:
                msubs.append((sub, sbufs, 0))
        main = [(s_, b_) for (s_, b_) in subs if s_.main]

        def tail(sub, sbufs, j=j):
            final_out(sub, sbufs)
            if j + 1 < ntiles:
                nsub = Sub(sub.name, sub.c0, sub.n, sub.tok0 + TT, True)
                load_x(nsub, sbufs)
                return lambda: norm_to_h(nsub, sbufs, V_FFN1)
            return None

        if debug_stop is None:
            mixer(msubs, first_of_core=(j == 0), hook=lambda sub, sbufs: (lambda: norm_to_h_2(sub, sbufs, V_FFN2)))
            ffn("ffn2", main, hook=tail)
        else:
            mixer(msubs, first_of_core=(j == 0), hook=lambda sub, sbufs: None)
            for sp_ in wspecs_ffn("ffn2"):
                wtile(sp_[0])
            for (sub, sbufs) in main:
                tail(sub, sbufs)
    flush()
    assert wstate["cons"] == len(wlist)
    S.emit(final_group=("sp", "o"))
    return nc, S


_CACHE = {}


def _pack_vecs(ffn1_norm, mix_norm, ffn2_norm, final_norm, conv_dw_b, conv_ln_g, conv_ln_b, pool_scale, conv_dw):
    v = np.zeros((128, NV), np.float32)

    def put(col, a):
        a = np.asarray(a, np.float32)
        k = a.shape[0] // 128
        v[:, col:col + k] = a.reshape(k, 128).T
    put(V_FFN1, ffn1_norm)
    put(V_MIX, mix_norm)
    put(V_FFN2, ffn2_norm)
    put(V_FIN, final_norm)
    put(V_CB, conv_dw_b)
    put(V_LNG, conv_ln_g)
    put(V_LNB, conv_ln_b)
    put(V_PSC, pool_scale)
    dw = np.asarray(conv_dw, np.float32)
    dwp = np.zeros((2 * NPAIR, D_CONV), np.float32)
    dwp[:CONV_W] = dw
    for c in range(CC):
        for hh in range(2):
            ch = dwp[:, c * 128 + 64 * hh:c * 128 + 64 * hh + 64]
            ev, od = ch[0::2].T, ch[1::2].T
            col = V_DW + (c * 2 + hh) * NPAIR
            if hh == 0:
                v[0:64, col:col + NPAIR], v[64:128, col:col + NPAIR] = ev, od
            else:
                v[0:64, col:col + NPAIR], v[64:128, col:col + NPAIR] = od, ev
    return v


def kernel(x, ffn1_norm, ffn1_w_gate, ffn1_w_up, ffn1_w_down, mix_norm, w_in,
           conv_dw, conv_dw_b, conv_ln_g, conv_ln_b, conv_pw, pool_w, pool_scale,
           w_out, ffn2_norm, ffn2_w_gate, ffn2_w_up, ffn2_w_down, final_norm):
    x = np.asarray(x, np.float32)
    Bn, T, Dm = x.shape
    n_cores = 8
    per_b = n_cores // Bn
    TOK = T // per_b
    ntiles = TOK // TT
    key = ntiles
    if key not in _CACHE:
        _CACHE[key] = build_program(ntiles)[0]
    nc = _CACHE[key]
    f32c = lambda a: np.ascontiguousarray(np.asarray(a, np.float32))
    shared = {
        "ident": np.ascontiguousarray(np.tile(np.eye(64, dtype=np.float32), (2, 1))),
        "vecs": _pack_vecs(ffn1_norm, mix_norm, ffn2_norm, final_norm, conv_dw_b, conv_ln_g, conv_ln_b,
                           pool_scale, conv_dw),
        "ffn1_w_gate": f32c(ffn1_w_gate), "ffn1_w_up": f32c(ffn1_w_up), "ffn1_w_down": f32c(ffn1_w_down),
        "ffn2_w_gate": f32c(ffn2_w_gate), "ffn2_w_up": f32c(ffn2_w_up), "ffn2_w_down": f32c(ffn2_w_down),
        "w_in": f32c(w_in), "conv_pw": f32c(conv_pw), "pool_w": f32c(pool_w).reshape(4 * 128, 128),
        "w_out": f32c(w_out),
    }
    in_maps = []
    for i in range(n_cores):
        b, part = divmod(i, per_b)
        t0 = part * TOK
        xt = np.zeros((Dm, HALO + TOK), np.float32)
        xt[:, HALO:] = x[b, t0:t0 + TOK, :].T
        if part > 0:
            xt[:, :HALO] = x[b, t0 - HALO:t0, :].T
        pos = np.broadcast_to((t0 + np.arange(16, dtype=np.float32))[None, :], (128, 16))
        m = dict(shared)
        m["xT"] = xt
        m["pos"] = np.ascontiguousarray(pos, dtype=np.float32)
        in_maps.append(m)
    res = run_bass_kernel_spmd(nc, in_maps, core_ids=list(range(n_cores)))
    out = np.empty((Bn, T, Dm), np.float32)
    for i in range(n_cores):
        b, part = divmod(i, per_b)
        t0 = part * TOK
        out[b, t0:t0 + TOK, :] = np.asarray(res.results[i]["outT"], np.float32).T
    return out
```

```python
import numpy as np
import concourse.bass as bass
import concourse.mybir as mybir
from concourse.bass_utils import run_bass_kernel_spmd

F32 = mybir.dt.float32
BF16 = mybir.dt.bfloat16
AF = mybir.ActivationFunctionType
ALU = mybir.AluOpType

D_MODEL = 1024
D_FF = 2816
D_CONV = 512
CONV_W = 31
KC = 8
FCH = 11
CC = 4
HALO = 32
SUBW = 512
TT = 1024
NSLOT = 6
RSQ = "lnexp"
RMS_EPS = 1e-6
LN_EPS = 1e-5

V_FFN1, V_MIX, V_FFN2, V_FIN = 0, 8, 16, 24
V_CB, V_LNG, V_LNB, V_PSC = 32, 36, 40, 44
V_DW = 48
NPAIR = 16
NV = V_DW + CC * 2 * NPAIR

COMPUTE = ("pe", "act", "dve", "pool")


class Buf:
    __slots__ = ("name", "lastw", "readers", "psum")

    def __init__(self, name, psum=False):
        self.name = name
        self.lastw = None
        self.readers = []
        self.psum = psum


class Sched:
    def __init__(self, nc, n_dma_sems=8):
        self.nc = nc
        self.prog = {e: [] for e in COMPUTE + ("sp",)}
        self.sem = {e: nc.alloc_semaphore(f"s_{e}") for e in COMPUTE}
        self.cnt = {e: 0 for e in COMPUTE}
        self.waited = {e: {} for e in COMPUTE + ("sp",)}
        self.dma_sems = {}
        self.n_dma_sems = n_dma_sems
        self.dma_rr = {}

    def _need(self, eng, tok, waits):
        if tok is None:
            return
        key, sem, val = tok
        if self.waited[eng].get(key, 0) >= val:
            return
        self.waited[eng][key] = val
        waits[key] = (sem, val)

    def _deps(self, eng, reads, writes):
        waits = {}
        for b in reads:
            self._need(eng, b.lastw, waits)
            if b.psum:
                for t in b.readers:
                    if t[0] != eng:
                        self._need(eng, t, waits)
        for b in writes:
            self._need(eng, b.lastw, waits)
            for t in b.readers:
                self._need(eng, t, waits)
        return list(waits.values())

    def _commit(self, tok, reads, writes):
        for b in reads:
            b.readers.append(tok)
            if len(b.readers) > 12:
                best = {}
                for t in b.readers:
                    if t[0] not in best or best[t[0]][2] < t[2]:
                        best[t[0]] = t
                b.readers = list(best.values())
        for b in writes:
            b.lastw = tok
            b.readers = []

    def op(self, eng, fn, reads=(), writes=()):
        waits = self._deps(eng, reads, writes)
        self.cnt[eng] += 1
        tok = (eng, self.sem[eng], self.cnt[eng])
        self.prog[eng].append((waits, fn, self.sem[eng]))
        self._commit(tok, reads, writes)
        return tok

    def dma(self, eng, fn, reads=(), writes=(), group="g"):
        k = (eng, group)
        if k not in self.dma_sems:
            self.dma_sems[k] = [[self.nc.alloc_semaphore(f"d_{eng}_{group}_{i}"), 0]
                                for i in range(self.n_dma_sems)]
            self.dma_rr[k] = 0
        i = self.dma_rr[k]
        self.dma_rr[k] = (i + 1) % len(self.dma_sems[k])
        ent = self.dma_sems[k][i]
        sem, prev = ent
        waits = self._deps(eng, reads, writes)
        if prev > 0:
            w2 = {}
            self._need(eng, ((eng, group, i), sem, prev), w2)
            waits += list(w2.values())
        ent[1] = prev + 16
        tok = ((eng, group, i), sem, prev + 16)
        self.prog[eng].append((waits, fn, ("dma", sem)))
        self._commit(tok, reads, writes)
        return tok

    def emit(self, final_group=None):
        names = {"pe": "tensor", "act": "scalar", "dve": "vector", "pool": "gpsimd", "sp": "sync"}
        final_waits = []
        if final_group is not None:
            final_waits = [(None, sem, val) for (sem, val) in self.dma_sems[final_group] if val > 0]
        with self.nc.Block() as block:
            for e, attr in names.items():
                prog = self.prog[e]
                fw = final_waits if e == "sp" else ()

                def section(engine, prog=prog, fw=fw):
                    for waits, fn, sem in prog:
                        for (s, v) in waits:
                            engine.wait_ge(s, v)
                        ins = fn(engine)
                        if isinstance(sem, tuple):
                            ins.then_inc(sem[1], 16)
                        else:
                            ins.then_inc(sem, 1)
                    for (_k, s, v) in fw:
                        engine.wait_ge(s, v)

                getattr(block, attr)(section)


class Sub:
    def __init__(self, name, c0, n, tok0, main):
        self.name, self.c0, self.n, self.tok0, self.main = name, c0, n, tok0, main
        self.sl = slice(c0, c0 + n)


def build_program(ntiles=4, bcast_diag=True, debug_stop=None):
    TOK = ntiles * TT
    nc = bass.Bass("TRN2", target_bir_lowering=False)
    S = Sched(nc)

    def din(name, shape):
        return nc.dram_tensor(name, list(shape), F32, kind="ExternalInput").ap()

    xT_d = din("xT", [D_MODEL, HALO + TOK])
    pos_d = din("pos", [128, 16])
    vecs_d = din("vecs", [128, NV])
    ident_d = din("ident", [128, 64])
    W = {}
    for f in ("ffn1", "ffn2"):
        W[f + "_g"] = din(f + "_w_gate", [D_MODEL, D_FF])
        W[f + "_u"] = din(f + "_w_up", [D_MODEL, D_FF])
        W[f + "_d"] = din(f + "_w_down", [D_FF, D_MODEL])
    W["w_in"] = din("w_in", [D_MODEL, 3 * D_CONV])
    W["conv_pw"] = din("conv_pw", [D_CONV, D_CONV])
    W["pool_w"] = din("pool_w", [4 * 128, 128])
    W["w_out"] = din("w_out", [D_MODEL, D_MODEL])
    outT_d = nc.dram_tensor("outT", [D_MODEL, TOK], F32, kind="ExternalOutput").ap()

    WCOLS = HALO + TT
    sb = nc.alloc_sbuf_tensor
    xT = sb("xT_sb", [128, KC, WCOLS], F32)
    hT = sb("hT_sb", [128, KC, WCOLS], BF16)
    actT = sb("actT_sb", [128, FCH, WCOLS], BF16)
    wring = [sb(f"wr{i}", [128, 8, 512], BF16) for i in range(NSLOT)]
    wring_b = [Buf(f"wr{i}") for i in range(NSLOT)]
    NBF = 4
    bfr = [sb(f"bfr{i}", [128, SUBW], BF16) for i in range(NBF)]
    bfr_b = [Buf(f"bfr{i}") for i in range(NBF)]
    NTMP = 3
    tmpf = [sb(f"tmpf{i}", [128, SUBW], F32) for i in range(NTMP)]
    tmpf_b = [Buf(f"tmpf{i}") for i in range(NTMP)]
    NST = 2
    stt = [sb(f"stt{i}", [128, SUBW], F32) for i in range(NST)]
    stt_b = [Buf(f"stt{i}") for i in range(NST)]
    lnM = sb("lnM", [128, SUBW], F32)
    lnV = sb("lnV", [128, SUBW], F32)
    lnM_b, lnV_b = Buf("lnM"), Buf("lnV")
    ubuf = [sb(f"ubuf{i}", [128, CC, 2, HALO + SUBW], BF16) for i in range(2)]
    ubuf_b = [[Buf(f"u{i}_{c}") for c in range(CC)] for i in range(2)]
    urep_b = [[Buf(f"ur{i}_{c}") for c in range(CC)] for i in range(2)]
    uhist_b = [Buf(f"uh{i}") for i in range(2)]
    pbuf = [sb(f"pbuf{i}", [128, CC, 16 + SUBW], F32) for i in range(2)]
    pbuf_b = [[Buf(f"p{i}_{c}") for c in range(CC)] for i in range(2)]
    phist_b = [Buf(f"ph{i}") for i in range(2)]
    psc = [sb(f"psc{i}", [128, 16 + SUBW], F32) for i in range(2)]
    psc_b = [Buf(f"psc{i}") for i in range(2)]
    convo = sb("convo", [128, CC, SUBW], F32)
    convo_b = [Buf(f"convo{c}") for c in range(CC)]
    yb = sb("yb", [128, CC, SUBW], BF16)
    yb_b = [Buf(f"yb{c}") for c in range(CC)]
    NMX = 4
    mixed = [sb(f"mixed{i}", [128, SUBW], BF16) for i in range(NMX)]
    mixed_b = [Buf(f"mixed{i}") for i in range(NMX)]
    catT = sb("catT", [128, 2 * CC, SUBW], BF16)
    catT_b = [Buf(f"cat{c}") for c in range(2 * CC)]
    NDG = 3
    NDGD, NDGP = 2, 4
    dgr = [sb(f"dgr{i}", [128, 8, 64], BF16) for i in range(NDGD)]
    dgr_b = [Buf(f"dgr{i}") for i in range(NDGD)]
    dgp = [sb(f"dgp{i}", [128, 8, 64], BF16) for i in range(NDGP)]
    dgp_b = [Buf(f"dgp{i}") for i in range(NDGP)]
    vecs = sb("vecs_sb", [128, NV], F32)
    vecs_b = Buf("vecs")
    wh = sb("wh_sb", [128, CC * 2 * NPAIR], F32)
    wh_b = Buf("wh")
    ones = sb("ones_sb", [128, 128], BF16)
    ones_b = Buf("ones")
    identf = sb("identf_sb", [128, 64], F32)
    ident_b = Buf("ident")
    epsc = sb("epsc_sb", [128, 2], F32)
    negh_b = Buf("epsc")
    rcnt = sb("rcnt_sb", [128, CC, 16], F32)
    rcnt_b = Buf("rcnt")

    NBANK = 8
    banks = [nc.alloc_psum_tensor(f"bank{i}", [128, SUBW], F32) for i in range(NBANK)]
    banks_b = [Buf(f"bank{i}", psum=True) for i in range(NBANK)]
    rr = {"bank": 0, "bfr": 0, "tmp": 0, "stt": 0, "mixed": 0, "dgr": 0, "dgp": 0, "psc": 0}

    def nxt(kind, n):
        i = rr[kind]
        rr[kind] = (i + 1) % n
        return i

    held = set()

    def bank(hold=False):
        for _ in range(NBANK):
            i = nxt("bank", NBANK)
            if i not in held:
                break
        else:
            raise RuntimeError("all PSUM banks held")
        if hold:
            held.add(i)
        return banks[i], banks_b[i]

    def release(bk_b):
        held.discard(banks_b.index(bk_b))

    S.dma("sp", lambda e: e.dma_start(out=vecs[:], in_=vecs_d), writes=[vecs_b], group="c")
    posb, pos_b = lnV[:, 0:16], lnV_b
    fix, fix_b = lnM[:, 0:16], lnM_b
    S.dma("sp", lambda e: e.dma_start(out=posb, in_=pos_d), writes=[pos_b], group="c")
    S.op("dve", lambda v: v.memset(ones[:], 1.0), writes=[ones_b])
    S.op("dve", lambda v: v.memset(epsc[:, 0:1], RMS_EPS), writes=[negh_b])
    S.op("dve", lambda v: v.memset(epsc[:, 1:2], LN_EPS), reads=[negh_b], writes=[negh_b])
    S.dma("sp", lambda e: e.dma_start(out=identf[:], in_=ident_d), writes=[ident_b], group="c")
    for i in range(2):
        S.op("dve", lambda v, i=i: v.memset(psc[i][:], 0.0), writes=[psc_b[i]])
        S.op("pool", lambda g, i=i: g.memset(ubuf[i][:], 0.0),
             writes=ubuf_b[i] + urep_b[i] + [uhist_b[i]])
    S.op("dve", lambda v: v.tensor_scalar(out=wh[:], in0=vecs[:, V_DW:V_DW + CC * 2 * NPAIR], scalar1=0.5,
                                          scalar2=None, op0=ALU.mult), reads=[vecs_b], writes=[wh_b])
    for g in range(CC):
        S.op("dve", lambda v, g=g: v.tensor_scalar(out=rcnt[:, g, :], in0=posb, scalar1=1.0,
                                                   scalar2=float(2 ** (g + 1)), op0=ALU.add, op1=ALU.min),
             reads=[pos_b], writes=[rcnt_b])
    S.op("dve", lambda v: v.reciprocal(out=rcnt[:], in_=rcnt[:]), reads=[rcnt_b], writes=[rcnt_b])

    def wspecs_ffn(f):
        specs = []
        for half in range(2):
            f0 = half * FCH
            groups = [(g0, min(4, FCH - g0)) for g0 in range(0, FCH, 4)]
            for q, (g0, nf) in enumerate(groups):
                if half == 0 and q < 2:
                    rels = (4 - 2 * q, 3 - 2 * q)
                else:
                    rels = (2, 1)
                for nm, rel in zip(("_g", "_u"), rels):
                    specs.append(((f + nm, 0, 8, (f0 + g0) * 128, nf * 128), rel))
            for dg in range(2):
                for k0, r in ((0, 2), (8, 1)):
                    nk = min(8, FCH - k0)
                    rel = r if half == 0 else (4 - 2 * dg - (0 if k0 == 0 else 1))
                    specs.append(((f + "_d", (f0 + k0) * 128, nk, dg * 512, 512), rel))
        return specs

    def wspecs_mixer():
        return [(("w_in", 0, 8, 0, 512), 3), (("w_in", 0, 8, 512, 512), 2), (("w_in", 0, 8, 1024, 512), 1),
                (("conv_pw", 0, 4, 0, 512), 4), (("pool_w", 0, 4, 0, 128), 3),
                (("w_out", 0, 8, 0, 512), 2), (("w_out", 0, 8, 512, 512), 1)]

    wlist = []
    for _j in range(ntiles):
        wlist += wspecs_ffn("ffn1") + wspecs_mixer() + wspecs_ffn("ffn2")
    wstate = {"loaded": 0, "cons": 0}
    startup_gate = []

    def w_issue(i):
        name, r0, nk, c0, ncols = wlist[i][0]
        slot = i % NSLOT
        src = W[name][r0:r0 + nk * 128, c0:c0 + ncols].rearrange("(k p) c -> p k c", p=128)
        dst = wring[slot][:, 0:nk, 0:ncols]
        gate = list(startup_gate) if 4 <= i < NSLOT else []
        S.dma("pool", lambda e, dst=dst, src=src: e.dma_start(out=dst, in_=src),
              reads=gate, writes=[wring_b[slot]], group="w")

    def wtile(spec):
        i = wstate["cons"]
        assert wlist[i][0] == spec, (i, wlist[i], spec)
        while wstate["loaded"] < min(len(wlist), i + NSLOT):
            m = wstate["loaded"]
            if m >= NSLOT:
                k = m - NSLOT
                if k + wlist[k][1] > i:
                    break
            w_issue(m)
            wstate["loaded"] += 1
        assert wstate["loaded"] > i
        wstate["cons"] += 1
        return wring[i % NSLOT], wring_b[i % NSLOT]

    class SB:
        pass

    def mk_sub_bufs(sub):
        o = SB()
        o.x = [Buf(f"x{c}{sub.name}") for c in range(KC)]
        o.h = [Buf(f"h{c}{sub.name}") for c in range(KC)]
        o.a = [Buf(f"a{c}{sub.name}") for c in range(FCH)]
        return o

    def mm_group(ps_ap, pairs, reads, bank_b):
        def fn(t, pairs=pairs, ps_ap=ps_ap):
            ins = None
            n = len(pairs)
            for i, (l, r) in enumerate(pairs):
                ins = t.matmul(ps_ap, lhsT=l, rhs=r, start=(i == 0), stop=(i == n - 1))
            return ins
        S.op("pe", fn, reads=reads, writes=[bank_b])

    def norm_sums(sub, sbufs):
        n = sub.n
        bk, bk_b = bank(hold=True)
        for c in range(KC):
            i = nxt("bfr", NBF)
            S.op("act", lambda a, i=i, c=c: a.activation(out=bfr[i][:, 0:n], in_=xT[:, c, sub.sl], func=AF.Square),
                 reads=[sbufs.x[c]], writes=[bfr_b[i]])
            S.op("pe", lambda t, i=i, c=c: t.matmul(bk[:, 0:n], lhsT=ones[:], rhs=bfr[i][:, 0:n],
                                                    start=(c == 0), stop=(c == KC - 1)),
                 reads=[bfr_b[i], ones_b], writes=[bk_b])
        return bk, bk_b

    def norm_rstd(sub, bk, bk_b):
        n = sub.n
        si = nxt("stt", NST)
        S.op("act", lambda a: a.activation(out=stt[si][:, 0:n], in_=bk[:, 0:n], func=AF.Ln,
                                           scale=1.0 / D_MODEL, bias=epsc[:, 0:1]),
             reads=[bk_b, negh_b], writes=[stt_b[si]])
        release(bk_b)
        S.op("act", lambda a: a.activation(out=stt[si][:, 0:n], in_=stt[si][:, 0:n], func=AF.Exp, scale=-0.5),
             reads=[stt_b[si]], writes=[stt_b[si]])
        return stt[si], stt_b[si]

    def norm_stats(sub, sbufs):
        bk, bk_b = norm_sums(sub, sbufs)
        return norm_rstd(sub, bk, bk_b)

    def h_ops(sub, sbufs, vcol, r, r_b):
        n = sub.n
        for c in range(KC):
            S.op("dve", lambda v, c=c: v.scalar_tensor_tensor(out=hT[:, c, sub.sl], in0=xT[:, c, sub.sl],
                                                              scalar=vecs[:, vcol + c:vcol + c + 1], in1=r[:, 0:n],
                                                              op0=ALU.mult, op1=ALU.mult),
                 reads=[sbufs.x[c], r_b, vecs_b], writes=[sbufs.h[c]])

    def norm_to_h_2(sub, sbufs, vcol):
        bk, bk_b = norm_sums(sub, sbufs)

        def part2():
            r, r_b = norm_rstd(sub, bk, bk_b)
            h_ops(sub, sbufs, vcol, r, r_b)
        return part2

    def norm_to_h(sub, sbufs, vcol):
        r, r_b = norm_stats(sub, sbufs)
        h_ops(sub, sbufs, vcol, r, r_b)

    pending = []

    def flush():
        while pending:
            r_ = pending.pop(0)()
            if callable(r_):
                r_()

    def ffn(f, subs, hook=None):
        def up_chunk(sub, sbufs, wg, wg_b, wu, wu_b, fi, fl):
            n = sub.n
            bg, bg_b = bank()
            mm_group(bg[:, 0:n], [(wg[:, k, fi * 128:(fi + 1) * 128], hT[:, k, sub.sl]) for k in range(KC)],
                     [wg_b] + sbufs.h, bg_b)
            bu, bu_b = bank()
            mm_group(bu[:, 0:n], [(wu[:, k, fi * 128:(fi + 1) * 128], hT[:, k, sub.sl]) for k in range(KC)],
                     [wu_b] + sbufs.h, bu_b)
            ti = nxt("tmp", NTMP)
            S.op("act", lambda a: a.activation(out=tmpf[ti][:, 0:n], in_=bg[:, 0:n], func=AF.Silu),
                 reads=[bg_b], writes=[tmpf_b[ti]])
            S.op("dve", lambda v: v.tensor_tensor(out=actT[:, fl, sub.sl], in0=bu[:, 0:n], in1=tmpf[ti][:, 0:n],
                                                  op=ALU.mult),
                 reads=[bu_b, tmpf_b[ti]], writes=[sbufs.a[fl]])

        def down_chunk(sub, sbufs, wd0, wd0_b, wd1, wd1_b, dg, dl):
            n = sub.n
            dc = dg * 4 + dl
            bk, bk_b = bank()
            pairs = []
            for fl in range(FCH):
                wt = wd0 if fl < 8 else wd1
                pairs.append((wt[:, fl % 8, dl * 128:(dl + 1) * 128], actT[:, fl, sub.sl]))
            mm_group(bk[:, 0:n], pairs, [wd0_b, wd1_b] + sbufs.a, bk_b)
            S.op("dve", lambda v: v.scalar_tensor_tensor(out=xT[:, dc, sub.sl], in0=bk[:, 0:n], scalar=0.5,
                                                         in1=xT[:, dc, sub.sl], op0=ALU.mult, op1=ALU.add),
                 reads=[bk_b, sbufs.x[dc]], writes=[sbufs.x[dc]])

        for half in range(2):
            f0 = half * FCH
            groups = []
            for g0 in range(0, FCH, 4):
                nf = min(4, FCH - g0)
                groups.append((g0, nf))
            gi = 0
            if half == 0:
                tl = []
                for (g0, nf) in groups[:2]:
                    tl.append(wtile((f + "_g", 0, 8, (f0 + g0) * 128, nf * 128)))
                    tl.append(wtile((f + "_u", 0, 8, (f0 + g0) * 128, nf * 128)))
                for (sub, sbufs) in subs:
                    for q, (g0, nf) in enumerate(groups[:2]):
                        (wg, wg_b), (wu, wu_b) = tl[2 * q], tl[2 * q + 1]
                        for fi in range(nf):
                            up_chunk(sub, sbufs, wg, wg_b, wu, wu_b, fi, g0 + fi)
                            if sub.main and q == 1 and fi == 0:
                                flush()
                gi = 2
            for (g0, nf) in groups[gi:]:
                wg, wg_b = wtile((f + "_g", 0, 8, (f0 + g0) * 128, nf * 128))
                wu, wu_b = wtile((f + "_u", 0, 8, (f0 + g0) * 128, nf * 128))
                for (sub, sbufs) in subs:
                    for fi in range(nf):
                        up_chunk(sub, sbufs, wg, wg_b, wu, wu_b, fi, g0 + fi)
            if half == 0:
                for dg in range(2):
                    wd0, wd0_b = wtile((f + "_d", (f0 + 0) * 128, 8, dg * 512, 512))
                    wd1, wd1_b = wtile((f + "_d", (f0 + 8) * 128, FCH - 8, dg * 512, 512))
                    for (sub, sbufs) in subs:
                        for dl in range(4):
                            down_chunk(sub, sbufs, wd0, wd0_b, wd1, wd1_b, dg, dl)
            else:
                tl = []
                for dg in range(2):
                    tl.append(wtile((f + "_d", (f0 + 0) * 128, 8, dg * 512, 512)))
                    tl.append(wtile((f + "_d", (f0 + 8) * 128, FCH - 8, dg * 512, 512)))
                for (sub, sbufs) in subs:
                    for dg in range(2):
                        (wd0, wd0_b), (wd1, wd1_b) = tl[2 * dg], tl[2 * dg + 1]
                        for dl in range(4):
                            down_chunk(sub, sbufs, wd0, wd0_b, wd1, wd1_b, dg, dl)
                            if dg == 1 and dl == 1 and sub.main:
                                flush()
                    if hook is not None:
                        p2 = hook(sub, sbufs)
                        if p2 is not None:
                            pending.append(p2)

    def mixer(subs, first_of_core, hook):
        wa, wa_b = wtile(("w_in", 0, 8, 0, 512))
        wg, wg_b = wtile(("w_in", 0, 8, 512, 512))
        wp, wp_b = wtile(("w_in", 0, 8, 1024, 512))
        first_main = [True]
        deferred = []

        def hist_copy(ui):
            nu = 1 - ui
            S.op("pool", lambda g: g.tensor_copy(out=ubuf[nu][0:64, :, 0, 0:HALO],
                                                 in_=ubuf[ui][0:64, :, 0, SUBW:SUBW + HALO]),
                 reads=ubuf_b[ui], writes=[uhist_b[nu]])
            S.op("pool", lambda g: g.tensor_copy(out=ubuf[nu][64:128, :, 1, 0:HALO],
                                                 in_=ubuf[ui][64:128, :, 1, SUBW:SUBW + HALO]),
                 reads=ubuf_b[ui], writes=[uhist_b[nu]])
            S.op("pool", lambda g: g.tensor_copy(out=pbuf[nu][:, :, 0:16], in_=pbuf[ui][:, :, SUBW:SUBW + 16]),
                 reads=pbuf_b[ui], writes=[phist_b[nu]])

        msubs = [s_ for s_ in subs if s_[0].main]
        sts = [MixState(sub, sbufs, ui) for (sub, sbufs, ui) in msubs]
        A_, B_ = sts
        for (sub, sbufs, ui) in subs:
            n = sub.n
            ucol = HALO if sub.main else 0
            for c in range(CC):
                ba, ba_b = bank()
                mm_group(ba[:, 0:n], [(wa[:, k, c * 128:(c + 1) * 128], hT[:, k, sub.sl]) for k in range(KC)],
                         [wa_b] + sbufs.h, ba_b)
                bg, bg_b = bank()
                mm_group(bg[:, 0:n], [(wg[:, k, c * 128:(c + 1) * 128], hT[:, k, sub.sl]) for k in range(KC)],
                         [wg_b] + sbufs.h, bg_b)
                ti = nxt("tmp", NTMP)
                S.op("act", lambda a, ti=ti, bg=bg, n=n: a.activation(out=tmpf[ti][:, 0:n], in_=bg[:, 0:n],
                                                                      func=AF.Tanh, scale=0.5),
                     reads=[bg_b], writes=[tmpf_b[ti]])
                uw = [ubuf_b[ui][c]] if sub.main else [uhist_b[ui]]
                for hh in range(2):
                    ps_ = slice(64 * hh, 64 * hh + 64)
                    S.op("dve", lambda v, ti=ti, ba=ba, n=n, c=c, ui=ui, ucol=ucol, hh=hh, ps_=ps_: v.scalar_tensor_tensor(
                        out=ubuf[ui][ps_, c, hh, ucol:ucol + n], in0=tmpf[ti][ps_, 0:n], scalar=1.0, in1=ba[ps_, 0:n],
                        op0=ALU.add, op1=ALU.mult),
                         reads=[ba_b, tmpf_b[ti]], writes=uw)
                if sub.main:
                    WU = HALO + SUBW
                    S.dma("sp", lambda e, c=c, ui=ui: e.dma_start(out=ubuf[ui][64:128, c, 0, 0:WU - 1],
                                                                  in_=ubuf[ui][0:64, c, 0, 1:WU]),
                          reads=[ubuf_b[ui][c], uhist_b[ui]], writes=[urep_b[ui][c]], group="r")
                    S.dma("sp", lambda e, c=c, ui=ui: e.dma_start(out=ubuf[ui][0:64, c, 1, 0:WU - 1],
                                                                  in_=ubuf[ui][64:128, c, 1, 1:WU]),
                          reads=[ubuf_b[ui][c], uhist_b[ui], urep_b[ui][c]], writes=[urep_b[ui][c]], group="r")
                if sub.main and first_main[0] and c == 1:
                    flush()
            for c in range(CC):
                bp, bp_b = bank()
                mm_group(bp[:, 0:n], [(wp[:, k, c * 128:(c + 1) * 128], hT[:, k, sub.sl]) for k in range(KC)],
                         [wp_b] + sbufs.h, bp_b)
                if sub.main:
                    S.op("act", lambda a, bp=bp, c=c, ui=ui, n=n: a.activation(
                        out=pbuf[ui][:, c, 16:16 + n], in_=bp[:, 0:n], func=AF.Identity),
                         reads=[bp_b], writes=[pbuf_b[ui][c]])
                else:
                    S.op("act", lambda a, bp=bp, c=c, ui=ui: a.activation(
                        out=pbuf[ui][:, c, 0:16], in_=bp[:, 16:32], func=AF.Identity),
                         reads=[bp_b], writes=[phist_b[ui]])
            if sub.main:
                if first_main[0]:
                    hist_copy(ui)
                    first_main[0] = False
                    pool_dve(A_, first_of_core, eng="pool")
                else:
                    deferred.append(ui)
        wpw, wpw_b = wtile(("conv_pw", 0, 4, 0, 512))
        wpl, wpl_b = wtile(("pool_w", 0, 4, 0, 128))
        wos = [wtile(("w_out", 0, 8, 0, 512)), wtile(("w_out", 0, 8, 512, 512))]
        for c in range(CC):
            conv_chunk(A_, c)
        B_.all_pool = True
        conv_chunk(B_, 0)
        evac_part(A_)
        ln_apply(A_)
        conv_chunk(B_, 1)
        conv_chunk(B_, 2)
        conv_chunk(B_, 3)
        pw_stage(A_, wpw, wpw_b, evac=False)
        evac_part(B_)
        pw_evac(A_)
        pool_pe(A_, wpl, wpl_b)
        ln_apply(B_)
        pool_dve(B_, False, groups=(3,))
        wout_stage(A_, wos)
        pool_dve(B_, False, groups=(2, 1, 0))
        p2 = hook(A_.sub, A_.sbufs)
        while deferred:
            hist_copy(deferred.pop(0))
        pw_stage(B_, wpw, wpw_b)
        p2b = p2() if p2 is not None else None
        pool_pe(B_, wpl, wpl_b)
        if callable(p2b):
            p2b()
        wout_stage(B_, wos)
        p2 = hook(B_.sub, B_.sbufs)
        if p2 is not None:
            pending.append(p2)

    class MixState:
        def __init__(self, sub, sbufs, ui):
            self.sub, self.sbufs, self.ui, self.n = sub, sbufs, ui, sub.n
            self.cbanks = [None] * CC
            self.done_c = 0
            self.all_pool = False

    def conv_chunk(st, c):
        n, ui = st.n, st.ui
        bk, bk_b = bank(hold=True)
        st.cbanks[c] = (bk, bk_b)
        for r in range(2):
            tl = []
            for hh in range(2):
                on_pool = (st.all_pool and c >= 1) or (hh == 1)
                if on_pool:
                    di = nxt("dgp", NDGP)
                    dt_, dt_b = dgp[di], dgp_b[di]
                else:
                    di = nxt("dgr", NDGD)
                    dt_, dt_b = dgr[di], dgr_b[di]
                base = (c * 2 + hh) * NPAIR + r * 8
                wsl = wh[:, base:base + 8]
                S.op("pool" if on_pool else "dve", lambda v, dt_=dt_, wsl=wsl: v.tensor_tensor(
                    out=dt_[:], in0=identf[:].unsqueeze(1).to_broadcast([128, 8, 64]),
                    in1=wsl.unsqueeze(2).to_broadcast([128, 8, 64]), op=ALU.mult),
                     reads=[ident_b, wh_b], writes=[dt_b])
                tl.append((dt_, dt_b))

            def fn(t, tl=tl, r=r):
                ins = None
                for jl in range(8):
                    jj = r * 8 + jl
                    for hh in range(2):
                        ins = t.matmul(bk[64 * hh:64 * hh + 64, 0:n], lhsT=tl[hh][0][:, jl, :],
                                       rhs=ubuf[ui][:, c, hh, 2 + 2 * jj:2 + 2 * jj + n],
                                       start=(jj == 0), stop=(jj == NPAIR - 1), tile_position=(0, 64 * hh))
                return ins
            S.op("pe", fn, reads=[tl[0][1], tl[1][1], ubuf_b[ui][c], urep_b[ui][c], uhist_b[ui]], writes=[bk_b])

    def evac_part(st):
        n = st.n
        st.s1 = None
        st.bfi = []
        for c in range(CC):
            bk, bk_b = st.cbanks[c]
            bcol = vecs[:, V_CB + c:V_CB + c + 1]
            i1 = nxt("bfr", NBF)
            S.op("act", lambda a, bk=bk, i1=i1, bcol=bcol: a.activation(out=bfr[i1][:, 0:n], in_=bk[:, 0:n],
                                                                        func=AF.Identity, bias=bcol),
                 reads=[bk_b, vecs_b], writes=[bfr_b[i1]])
            i2 = nxt("bfr", NBF)
            S.op("act", lambda a, bk=bk, i2=i2, bcol=bcol: a.activation(out=bfr[i2][:, 0:n], in_=bk[:, 0:n],
                                                                        func=AF.Square, bias=bcol),
                 reads=[bk_b, vecs_b], writes=[bfr_b[i2]])
            S.op("dve", lambda v, bk=bk, c=c, bcol=bcol: v.tensor_scalar(out=convo[:, c, 0:n], in0=bk[:, 0:n],
                                                                         scalar1=bcol, scalar2=None, op0=ALU.add),
                 reads=[bk_b, vecs_b, bfr_b[i1], bfr_b[i2]], writes=[convo_b[c]])
            release(bk_b)
            st.bfi.append((i1, i2))
            if len(st.bfi) == 2 or c == CC - 1:
                if st.s1 is None:
                    st.s1, st.s1_b = bank(hold=True)
                    st.s2, st.s2_b = bank(hold=True)
                stats_pe(st)

    def stats_pe(st):
        n = st.n
        s1, s1_b, s2, s2_b = st.s1, st.s1_b, st.s2, st.s2_b
        while st.bfi:
            c = st.done_c
            i1, i2 = st.bfi.pop(0)
            S.op("pe", lambda t, i1=i1, c=c: t.matmul(s1[:, 0:n], lhsT=ones[:], rhs=bfr[i1][:, 0:n],
                                                      start=(c == 0), stop=(c == CC - 1)),
                 reads=[bfr_b[i1], ones_b], writes=[s1_b])
            S.op("pe", lambda t, i2=i2, c=c: t.matmul(s2[:, 0:n], lhsT=ones[:], rhs=bfr[i2][:, 0:n],
                                                      start=(c == 0), stop=(c == CC - 1)),
                 reads=[bfr_b[i2], ones_b], writes=[s2_b])
            st.done_c += 1

    def ln_apply(st):
        n = st.n
        s1, s1_b, s2, s2_b = st.s1, st.s1_b, st.s2, st.s2_b
        S.op("dve", lambda v: v.tensor_scalar(out=lnM[:, 0:n], in0=s1[:, 0:n], scalar1=1.0 / D_CONV, scalar2=None,
                                              op0=ALU.mult), reads=[s1_b], writes=[lnM_b])
        S.op("dve", lambda v: v.tensor_tensor(out=lnV[:, 0:n], in0=lnM[:, 0:n], in1=lnM[:, 0:n], op=ALU.mult),
             reads=[lnM_b], writes=[lnV_b])
        S.op("dve", lambda v: v.scalar_tensor_tensor(out=lnV[:, 0:n], in0=s2[:, 0:n], scalar=1.0 / D_CONV,
                                                     in1=lnV[:, 0:n], op0=ALU.mult, op1=ALU.subtract),
             reads=[s2_b, lnV_b], writes=[lnV_b])
        release(s1_b)
        release(s2_b)
        if RSQ == "lnexp":
            S.op("act", lambda a: a.activation(out=lnV[:, 0:n], in_=lnV[:, 0:n], func=AF.Ln, bias=epsc[:, 1:2]),
                 reads=[lnV_b, negh_b], writes=[lnV_b])
            S.op("act", lambda a: a.activation(out=lnV[:, 0:n], in_=lnV[:, 0:n], func=AF.Exp, scale=-0.5),
                 reads=[lnV_b], writes=[lnV_b])
        else:
            S.op("act", lambda a: a.activation(out=lnV[:, 0:n], in_=lnV[:, 0:n], func=AF.Sqrt, bias=epsc[:, 1:2]),
                 reads=[lnV_b, negh_b], writes=[lnV_b])
            S.op("dve", lambda v: v.reciprocal(out=lnV[:, 0:n], in_=lnV[:, 0:n]), reads=[lnV_b], writes=[lnV_b])
        S.op("dve", lambda v: v.scalar_tensor_tensor(out=lnM[:, 0:n], in0=lnM[:, 0:n], scalar=-1.0,
                                                     in1=lnV[:, 0:n], op0=ALU.mult, op1=ALU.mult),
             reads=[lnM_b, lnV_b], writes=[lnM_b])
        for c in range(CC):
            ti = nxt("tmp", NTMP)
            S.op("dve", lambda v, c=c, ti=ti: v.tensor_tensor(out=tmpf[ti][:, 0:n], in0=convo[:, c, 0:n],
                                                              in1=lnV[:, 0:n], op=ALU.mult),
                 reads=[convo_b[c], lnV_b], writes=[tmpf_b[ti]])
            S.op("dve", lambda v, ti=ti: v.tensor_tensor(out=tmpf[ti][:, 0:n], in0=tmpf[ti][:, 0:n],
                                                         in1=lnM[:, 0:n], op=ALU.add),
                 reads=[tmpf_b[ti], lnM_b], writes=[tmpf_b[ti]])
            S.op("act", lambda a, c=c, ti=ti: a.activation(out=yb[:, c, 0:n], in_=tmpf[ti][:, 0:n], func=AF.Silu,
                                                           scale=vecs[:, V_LNG + c:V_LNG + c + 1],
                                                           bias=vecs[:, V_LNB + c:V_LNB + c + 1]),
                 reads=[tmpf_b[ti], vecs_b], writes=[yb_b[c]])

    def pw_stage(st, wpw, wpw_b, evac=True):
        n = st.n
        st.pwb = []
        for co in range(CC):
            bk, bk_b = bank(hold=True)
            mm_group(bk[:, 0:n], [(wpw[:, k, co * 128:(co + 1) * 128], yb[:, k, 0:n]) for k in range(CC)],
                     [wpw_b] + yb_b, bk_b)
            st.pwb.append((bk, bk_b))
        if evac:
            pw_evac(st)

    def pw_evac(st):
        n = st.n
        for co, (bk, bk_b) in enumerate(st.pwb):
            S.op("act", lambda a, bk=bk, co=co: a.activation(out=catT[:, co, 0:n], in_=bk[:, 0:n], func=AF.Copy),
                 reads=[bk_b], writes=[catT_b[co]])
            release(bk_b)

    def pool_dve(st, fixup, groups=(3, 2, 1, 0), eng="dve"):
        n, ui = st.n, st.ui
        L = 16 + n
        if not hasattr(st, "mix"):
            st.mix = []
        for g in groups:
            src, src_b = pbuf[ui][:, g, :], [pbuf_b[ui][g], phist_b[ui]]
            sh = 1
            for _lvl in range(g + 1):
                pi = nxt("psc", 2)
                S.op(eng, lambda v, src=src, sh=sh, pi=pi: v.tensor_tensor(
                    out=psc[pi][:, sh:L], in0=src[:, sh:L], in1=src[:, 0:L - sh], op=ALU.add),
                     reads=src_b, writes=[psc_b[pi]])
                src, src_b = psc[pi], [psc_b[pi]]
                sh *= 2
            mi = nxt("mixed", NMX)
            wdt = float(2 ** (g + 1))
            S.op("dve", lambda v, src=src, mi=mi, g=g, wdt=wdt: v.scalar_tensor_tensor(
                out=mixed[mi][:, 0:n], in0=src[:, 16:16 + n], scalar=1.0 / wdt, in1=pbuf[ui][:, g, 16:16 + n],
                op0=ALU.mult, op1=ALU.subtract),
                 reads=src_b + [pbuf_b[ui][g]], writes=[mixed_b[mi]])
            if fixup:
                S.op("dve", lambda v, src=src, g=g: v.tensor_tensor(out=fix, in0=src[:, 16:32], in1=rcnt[:, g, :],
                                                                    op=ALU.mult),
                     reads=src_b + [rcnt_b], writes=[fix_b])
                S.op("dve", lambda v, mi=mi, g=g: v.tensor_tensor(out=mixed[mi][:, 0:16], in0=fix,
                                                                  in1=pbuf[ui][:, g, 16:32], op=ALU.subtract),
                     reads=[fix_b, pbuf_b[ui][g]], writes=[mixed_b[mi]])
            st.mix.append((g, mi))

    def pool_pe(st, wpl, wpl_b):
        n = st.n
        for (g, mi) in st.mix:
            bk, bk_b = bank()
            mm_group(bk[:, 0:n], [(wpl[:, g, 0:128], mixed[mi][:, 0:n])], [wpl_b, mixed_b[mi]], bk_b)
            S.op("act", lambda a, bk=bk, g=g: a.activation(out=catT[:, CC + g, 0:n], in_=bk[:, 0:n], func=AF.Identity,
                                                           scale=vecs[:, V_PSC + g:V_PSC + g + 1]),
                 reads=[bk_b, vecs_b], writes=[catT_b[CC + g]])

    def wout_stage(st, wos):
        n, sub, sbufs = st.n, st.sub, st.sbufs
        for dc in range(KC):
            wo, wo_b = wos[dc // 4]
            dl = dc % 4
            bk, bk_b = bank()
            mm_group(bk[:, 0:n], [(wo[:, k, dl * 128:(dl + 1) * 128], catT[:, k, 0:n]) for k in range(2 * CC)],
                     [wo_b] + catT_b, bk_b)
            S.op("dve", lambda v, bk=bk, dc=dc: v.tensor_tensor(
                out=xT[:, dc, sub.sl], in0=bk[:, 0:n], in1=xT[:, dc, sub.sl], op=ALU.add),
                 reads=[bk_b, sbufs.x[dc]], writes=[sbufs.x[dc]])

    out_tokens = []

    def final_out(sub, sbufs):
        n = sub.n
        if debug_stop is not None:
            for c in range(KC):
                tok = S.dma("sp", lambda e, c=c: e.dma_start(out=outT_d[c * 128:(c + 1) * 128, sub.tok0:sub.tok0 + n],
                                                             in_=xT[:, c, sub.sl]),
                            reads=[sbufs.x[c]], group="o")
            return
        r, r_b = norm_stats(sub, sbufs)
        for c in range(KC):
            S.op("dve", lambda v, c=c: v.scalar_tensor_tensor(out=xT[:, c, sub.sl], in0=xT[:, c, sub.sl],
                                                              scalar=vecs[:, V_FIN + c:V_FIN + c + 1], in1=r[:, 0:n],
                                                              op0=ALU.mult, op1=ALU.mult),
                 reads=[sbufs.x[c], r_b, vecs_b], writes=[sbufs.x[c]])
            S.dma("sp", lambda e, c=c: e.dma_start(out=outT_d[c * 128:(c + 1) * 128, sub.tok0:sub.tok0 + n],
                                                   in_=xT[:, c, sub.sl]),
                  reads=[sbufs.x[c]], group="o")

    gsub = 0
    slotA = Sub("A", HALO, SUBW, 0, True)
    slotB = Sub("B", HALO + SUBW, SUBW, 0, True)
    slotH = Sub("H", 0, HALO, 0, False)
    bufsA, bufsB, bufsH = mk_sub_bufs(slotA), mk_sub_bufs(slotB), mk_sub_bufs(slotH)
    def load_x(sub, sbufs):
        for c in range(KC):
            d0 = (HALO + sub.tok0) if sub.main else 0
            S.dma("sp", lambda e, c=c, d0=d0: e.dma_start(
                out=xT[:, c, sub.sl], in_=xT_d[c * 128:(c + 1) * 128, d0:d0 + sub.n]),
                  writes=[sbufs.x[c]], group="x")

    for j in range(ntiles):
        A = Sub("A", HALO, SUBW, j * TT, True)
        B = Sub("B", HALO + SUBW, SUBW, j * TT + SUBW, True)
        subs = [(A, bufsA), (B, bufsB)]
        if j == 0:
            subs = [(slotH, bufsH)] + subs
            for (sub, sbufs) in subs:
                load_x(sub, sbufs)
            startup_gate.extend(bufsA.x)
            for (sub, sbufs) in subs:
                norm_to_h(sub, sbufs, V_FFN1)
        ffn("ffn1", subs, hook=lambda sub, sbufs: (lambda: norm_to_h(sub, sbufs, V_MIX)))
        msubs = []
        for (sub, sbufs) in subs:
            if sub.main:
                msubs.append((sub, sbufs, gsub % 2))
                gsub += 1
            else:
                msubs.append((sub, sbufs, 0))
        main = [(s_, b_) for (s_, b_) in subs if s_.main]

        def tail(sub, sbufs, j=j):
            final_out(sub, sbufs)
            if j + 1 < ntiles:
                nsub = Sub(sub.name, sub.c0, sub.n, sub.tok0 + TT, True)
                load_x(nsub, sbufs)
                return lambda: norm_to_h(nsub, sbufs, V_FFN1)
            return None

        if debug_stop is None:
            mixer(msubs, first_of_core=(j == 0), hook=lambda sub, sbufs: (lambda: norm_to_h_2(sub, sbufs, V_FFN2)))
            ffn("ffn2", main, hook=tail)
        else:
            mixer(msubs, first_of_core=(j == 0), hook=lambda sub, sbufs: None)
            for sp_ in wspecs_ffn("ffn2"):
                wtile(sp_[0])
            for (sub, sbufs) in main:
                tail(sub, sbufs)
    flush()
    assert wstate["cons"] == len(wlist)
    S.emit(final_group=("sp", "o"))
    return nc, S


_CACHE = {}


def _pack_vecs(ffn1_norm, mix_norm, ffn2_norm, final_norm, conv_dw_b, conv_ln_g, conv_ln_b, pool_scale, conv_dw):
    v = np.zeros((128, NV), np.float32)

    def put(col, a):
        a = np.asarray(a, np.float32)
        k = a.shape[0] // 128
        v[:, col:col + k] = a.reshape(k, 128).T
    put(V_FFN1, ffn1_norm)
    put(V_MIX, mix_norm)
    put(V_FFN2, ffn2_norm)
    put(V_FIN, final_norm)
    put(V_CB, conv_dw_b)
    put(V_LNG, conv_ln_g)
    put(V_LNB, conv_ln_b)
    put(V_PSC, pool_scale)
    dw = np.asarray(conv_dw, np.float32)
    dwp = np.zeros((2 * NPAIR, D_CONV), np.float32)
    dwp[:CONV_W] = dw
    for c in range(CC):
        for hh in range(2):
            ch = dwp[:, c * 128 + 64 * hh:c * 128 + 64 * hh + 64]
            ev, od = ch[0::2].T, ch[1::2].T
            col = V_DW + (c * 2 + hh) * NPAIR
            if hh == 0:
                v[0:64, col:col + NPAIR], v[64:128, col:col + NPAIR] = ev, od
            else:
                v[0:64, col:col + NPAIR], v[64:128, col:col + NPAIR] = od, ev
    return v


def kernel(x, ffn1_norm, ffn1_w_gate, ffn1_w_up, ffn1_w_down, mix_norm, w_in,
           conv_dw, conv_dw_b, conv_ln_g, conv_ln_b, conv_pw, pool_w, pool_scale,
           w_out, ffn2_norm, ffn2_w_gate, ffn2_w_up, ffn2_w_down, final_norm):
    x = np.asarray(x, np.float32)
    Bn, T, Dm = x.shape
    n_cores = 8
    per_b = n_cores // Bn
    TOK = T // per_b
    ntiles = TOK // TT
    key = ntiles
    if key not in _CACHE:
        _CACHE[key] = build_program(ntiles)[0]
    nc = _CACHE[key]
    f32c = lambda a: np.ascontiguousarray(np.asarray(a, np.float32))
    shared = {
        "ident": np.ascontiguousarray(np.tile(np.eye(64, dtype=np.float32), (2, 1))),
        "vecs": _pack_vecs(ffn1_norm, mix_norm, ffn2_norm, final_norm, conv_dw_b, conv_ln_g, conv_ln_b,
                           pool_scale, conv_dw),
        "ffn1_w_gate": f32c(ffn1_w_gate), "ffn1_w_up": f32c(ffn1_w_up), "ffn1_w_down": f32c(ffn1_w_down),
        "ffn2_w_gate": f32c(ffn2_w_gate), "ffn2_w_up": f32c(ffn2_w_up), "ffn2_w_down": f32c(ffn2_w_down),
        "w_in": f32c(w_in), "conv_pw": f32c(conv_pw), "pool_w": f32c(pool_w).reshape(4 * 128, 128),
        "w_out": f32c(w_out),
    }
    in_maps = []
    for i in range(n_cores):
        b, part = divmod(i, per_b)
        t0 = part * TOK
        xt = np.zeros((Dm, HALO + TOK), np.float32)
        xt[:, HALO:] = x[b, t0:t0 + TOK, :].T
        if part > 0:
            xt[:, :HALO] = x[b, t0 - HALO:t0, :].T
        pos = np.broadcast_to((t0 + np.arange(16, dtype=np.float32))[None, :], (128, 16))
        m = dict(shared)
        m["xT"] = xt
        m["pos"] = np.ascontiguousarray(pos, dtype=np.float32)
        in_maps.append(m)
    res = run_bass_kernel_spmd(nc, in_maps, core_ids=list(range(n_cores)))
    out = np.empty((Bn, T, Dm), np.float32)
    for i in range(n_cores):
        b, part = divmod(i, per_b)
        t0 = part * TOK
        out[b, t0:t0 + TOK, :] = np.asarray(res.results[i]["outT"], np.float32).T
    return out
```
